# Optimizing a Trainium2 kernel written in Bass

```python
import math
import jax, jax.numpy as jnp
from jax import lax
import numpy as np

D_MODEL = 1024
BATCH = 8
SEQ = 2048
DEPTH = 4
DEC_BATCH = 128
DEC_SEQ = 4
PAST_LEN = 16384
PAGE_SIZE = 128

N_EVEN = (DEPTH + 1) // 2
N_ODD = DEPTH // 2
A_WIDTH = D_MODEL // 2
A_GROUP = 16
A_GROUPS = A_WIDTH // A_GROUP
A_STATE = 64
S5_MIN_STEP = 1e-3
S5_MAX_STEP = 1e-1
B_WIDTH = D_MODEL // 2
B_EXPAND = 128
B_HEADS = B_WIDTH // B_EXPAND
B_KDIM = B_EXPAND
B_VDIM = B_WIDTH // B_HEADS
B_CHUNK = 16
EVEN_SPLITS = tuple(int(s) for s in np.cumsum([A_WIDTH, A_WIDTH, B_WIDTH, B_WIDTH, B_WIDTH]))
EVEN_IN = 2 * A_WIDTH + 4 * B_WIDTH
C_HEAD = 64
C_HEADS = D_MODEL // C_HEAD
C_DECAY_LORA = max(32, int(round(1.8 * D_MODEL ** 0.5 / 32)) * 32)
C_AAA_LORA = max(32, int(round(1.8 * D_MODEL ** 0.5 / 32)) * 32)
C_MV_LORA = max(32, int(round(1.3 * D_MODEL ** 0.5 / 32)) * 32)
C_GATE_LORA = max(32, int(round(0.6 * D_MODEL ** 0.8 / 32)) * 32)
C_VRES_LAYERS = max(N_ODD - 1, 0)
DECAY_SCALE = math.exp(-0.5)
RMS_EPS = 1e-6
GN_EPS = 64e-5

kernel_name = 'hybrid_s5_hgrn2_rwkv7_step'


def _rmsnorm(x, w):
    x32 = x.astype(jnp.float32)
    return x32 * lax.rsqrt(jnp.mean(x32 * x32, axis=-1, keepdims=True) + RMS_EPS) * w.astype(jnp.float32)


def _complex_affine_combine(e1, e2):
    a1r, a1i, b1r, b1i = e1
    a2r, a2i, b2r, b2i = e2
    return (a2r * a1r - a2i * a1i,
            a2r * a1i + a2i * a1r,
            a2r * b1r - a2i * b1i + b2r,
            a2r * b1i + a2i * b1r + b2i)


def _s5(u, lam_re, lam_im, log_step, b_re, b_im, c_re, c_im, d, h0_re, h0_im):
    f32 = jnp.float32
    lr = jnp.minimum(lam_re.astype(f32), -1e-4)
    li = lam_im.astype(f32)
    step = jnp.exp(log_step.astype(f32))[:, None]
    mag = jnp.exp(lr * step)
    ab_re = mag * jnp.cos(li * step)
    ab_im = mag * jnp.sin(li * step)
    den = lr * lr + li * li
    nr = ab_re - 1.0
    cr = (nr * lr + ab_im * li) / den
    ci = (ab_im * lr - nr * li) / den
    b_re = b_re.astype(f32)
    b_im = b_im.astype(f32)
    bb_re = cr[..., None] * b_re - ci[..., None] * b_im
    bb_im = cr[..., None] * b_im + ci[..., None] * b_re
    bu_re = jnp.einsum('btgh,gph->btgp', u, bb_re)
    bu_im = jnp.einsum('btgh,gph->btgp', u, bb_im)
    h0_re = h0_re.astype(f32)
    h0_im = h0_im.astype(f32)
    bu_re = bu_re.at[:, 0].add(ab_re * h0_re - ab_im * h0_im)
    bu_im = bu_im.at[:, 0].add(ab_re * h0_im + ab_im * h0_re)
    a_re = jnp.broadcast_to(ab_re, bu_re.shape)
    a_im = jnp.broadcast_to(ab_im, bu_im.shape)
    _, _, h_re, h_im = lax.associative_scan(_complex_affine_combine, (a_re, a_im, bu_re, bu_im), axis=1)
    y = (jnp.einsum('btgp,ghp->btgh', h_re, c_re.astype(f32))
         - jnp.einsum('btgp,ghp->btgh', h_im, c_im.astype(f32))
         + d.astype(f32) * u)
    return y, h_re[:, -1], h_im[:, -1]


def _hgrn2_chunked(q, k, v, log_f, s0):
    bsz, t, h, _ = q.shape
    dv = v.shape[-1]
    c = math.gcd(t, B_CHUNK)
    n = t // c

    def to_chunks(a):
        return a.reshape(bsz, n, c, h, a.shape[-1]).transpose(1, 0, 3, 2, 4)

    causal = jnp.tril(jnp.ones((c, c), dtype=bool))

    def step(s, blk):
        qb, kb, vb, gb = blk
        b = jnp.cumsum(gb, axis=2)
        b_last = b[:, :, -1:]
        q_in = qb * jnp.exp(b)
        k_in = kb * jnp.exp(-b)
        k_end = kb * jnp.exp(b_last - b)
        att = jnp.where(causal, jnp.einsum('bhtk,bhsk->bhts', q_in, k_in), 0.0)
        o = jnp.einsum('bhts,bhsv->bhtv', att, vb) + jnp.einsum('bhtk,bhkv->bhtv', q_in, s)
        s_new = jnp.exp(b_last[:, :, 0])[..., None] * s + jnp.einsum('bhsk,bhsv->bhkv', k_end, vb)
        return s_new, o

    s_fin, oc = lax.scan(step, s0.astype(jnp.float32),
                         (to_chunks(q), to_chunks(k), to_chunks(v), to_chunks(log_f)))
    return oc.transpose(1, 0, 3, 2, 4).reshape(bsz, t, h, dv), s_fin


def _rwkv7_scan(r, decay, k, v, kk, a, s0):
    def step(s, inp):
        r_t, w_t, k_t, v_t, kk_t, a_t = inp
        s = (s * w_t[:, :, None, :]
             - jnp.einsum('bhvk,bhk->bhv', s, kk_t)[..., None] * (kk_t * a_t)[:, :, None, :]
             + v_t[..., None] * k_t[:, :, None, :])
        return s, jnp.einsum('bhvk,bhk->bhv', s, r_t)

    xs = tuple(jnp.swapaxes(z, 0, 1) for z in (r, decay, k, v, kk, a))
    s_fin, y = lax.scan(step, s0.astype(jnp.float32), xs)
    return jnp.swapaxes(y, 0, 1), s_fin


def _even_layer(xn, j, w, h0_re, h0_im, s0):
    bsz, t, _ = xn.shape
    proj = xn @ w['even_w_in'][j]
    u, z_a, q, f, i, z_b = jnp.split(proj, EVEN_SPLITS, axis=-1)
    y, h_re, h_im = _s5(u.reshape(bsz, t, A_GROUPS, A_GROUP),
                        w['ssm_lambda_re'][j], w['ssm_lambda_im'][j], w['ssm_log_step'][j],
                        w['ssm_b_re'][j], w['ssm_b_im'][j], w['ssm_c_re'][j], w['ssm_c_im'][j],
                        w['ssm_d'][j], h0_re, h0_im)
    y = jax.nn.gelu(y.reshape(bsz, t, A_WIDTH))
    y = y * jax.nn.sigmoid(y @ w['ssm_glu_w'][j] + w['ssm_glu_b'][j])
    out_a = y * jax.nn.silu(z_a)
    heads = lambda a: a.reshape(bsz, t, B_HEADS, -1)
    qh = heads(jax.nn.silu(q))
    fh = heads(f).astype(jnp.float32)
    ih = heads(i)
    lbs = jax.nn.softmax(w['hgrn_lower_bounds'].astype(jnp.float32), axis=0)
    lbs = jnp.cumsum(lbs, axis=0) - lbs[0]
    if j == 0:
        log_f = jax.nn.log_sigmoid(fh)
        kh = jax.nn.sigmoid(-fh)
    else:
        lb = lbs[j].reshape(B_HEADS, B_KDIM)
        log_f = jnp.log(lb + (1.0 - lb) * jax.nn.sigmoid(fh))
        kh = (1.0 - lb) * jax.nn.sigmoid(-fh)
    o, s_fin = _hgrn2_chunked(qh, kh, ih, log_f, s0)
    o = o * lax.rsqrt(jnp.mean(o * o, axis=-1, keepdims=True) + RMS_EPS) * w['hgrn_norm_w'][j]
    out_b = o.reshape(bsz, t, B_WIDTH) * jax.nn.silu(z_b)
    out = jnp.concatenate([out_a, out_b], axis=-1) @ w['even_w_out'][j]
    return out, h_re, h_im, s_fin


def _odd_layer(xn, j, w, shift_prev, s0, v_first):
    bsz, t, _ = xn.shape
    x_prev = jnp.concatenate([shift_prev.astype(jnp.float32)[:, None], xn[:, :-1]], axis=1)
    xx = x_prev - xn
    mix = w['rw_mix'][j]
    xr, xw, xk, xv, xa, xg = (xn + xx * mix[m] for m in range(6))
    wp = w['rw_w_rkvz'][j]
    r = xr @ wp[0]
    k = xk @ wp[1]
    v = xv @ wp[2]
    z = xg @ wp[3]
    decay = jnp.exp(-DECAY_SCALE * jax.nn.sigmoid(
        (w['rw_w0'][j] + jnp.tanh(xw @ w['rw_w1'][j]) @ w['rw_w2'][j]).astype(jnp.float32)))
    if v_first is None:
        v_first = v
    else:
        jv = j - 1
        v = v + (v_first - v) * jax.nn.sigmoid(w['rw_v0'][jv] + (xv @ w['rw_v1'][jv]) @ w['rw_v2'][jv])
    a = jax.nn.sigmoid(w['rw_a0'][j] + (xa @ w['rw_a1'][j]) @ w['rw_a2'][j])
    g = jax.nn.sigmoid(xg @ w['rw_g1'][j]) @ w['rw_g2'][j]
    heads = lambda a_: a_.reshape(bsz, t, C_HEADS, C_HEAD)
    kk = heads(k * w['rw_k_k'][j])
    kk = kk / jnp.maximum(jnp.sqrt(jnp.sum(kk * kk, axis=-1, keepdims=True)), 1e-12)
    k = k * (1.0 + (a - 1.0) * w['rw_k_a'][j])
    rh, kh, vh, ah, dh = heads(r), heads(k), heads(v), heads(a), heads(decay)
    y, s_fin = _rwkv7_scan(rh, dh, kh, vh, kk, ah, s0)
    mu = jnp.mean(y, axis=-1, keepdims=True)
    var = jnp.mean((y - mu) ** 2, axis=-1, keepdims=True)
    y = ((y - mu) * lax.rsqrt(var + GN_EPS)).reshape(bsz, t, D_MODEL) * w['rw_ln_w'][j] + w['rw_ln_b'][j]
    bonus = (jnp.sum(rh * kh * w['rw_r_k'][j], axis=-1, keepdims=True) * vh).reshape(bsz, t, D_MODEL)
    out = (y + bonus) * g * jax.nn.silu(z)
    return out @ w['rw_w_o'][j], s_fin, xn[:, -1], v_first


def _trunk(x, ssm_re, ssm_im, hgrn, wkv, shift, w):
    h = x.astype(jnp.float32)
    n_re, n_im, n_hg, n_wkv, n_sh = [], [], [], [], []
    v_first = None
    for layer in range(DEPTH):
        xn = _rmsnorm(h, w['norm_w'][layer])
        j = layer // 2
        if layer % 2 == 0:
            out, hr, hi, sb = _even_layer(xn, j, w, ssm_re[j], ssm_im[j], hgrn[j])
            n_re.append(hr)
            n_im.append(hi)
            n_hg.append(sb)
        else:
            out, sw, last, v_first = _odd_layer(xn, j, w, shift[j], wkv[j], v_first)
            n_wkv.append(sw)
            n_sh.append(last)
        h = h + out
    y = _rmsnorm(h, w['final_norm_w']).astype(x.dtype)
    return y, jnp.stack(n_re), jnp.stack(n_im), jnp.stack(n_hg), jnp.stack(n_wkv), jnp.stack(n_sh)


def setup_inputs(seed: int = 0) -> dict:
    key = jax.random.key(seed)
    ks = iter(jax.random.split(key, 64))
    f32 = jnp.float32
    D = D_MODEL
    G, H, P = A_GROUPS, A_GROUP, A_STATE

    def nrm(shape, scale=1.0):
        return scale * jax.random.normal(next(ks), shape, f32)

    def unif(shape, lo, hi):
        return jax.random.uniform(next(ks), shape, f32, lo, hi)

    return {
        'x_prompt': nrm((BATCH, SEQ, D)),
        'x_sample': nrm((DEC_BATCH, DEC_SEQ, D)),
        'state_ssm_re': nrm((N_EVEN, DEC_BATCH, G, P)),
        'state_ssm_im': nrm((N_EVEN, DEC_BATCH, G, P)),
        'state_hgrn': nrm((N_EVEN, DEC_BATCH, B_HEADS, B_KDIM, B_VDIM)),
        'state_wkv': nrm((N_ODD, DEC_BATCH, C_HEADS, C_HEAD, C_HEAD)),
        'state_shift': nrm((N_ODD, DEC_BATCH, D)),
        'norm_w': 1.0 + nrm((DEPTH, D), 0.02),
        'final_norm_w': 1.0 + nrm((D,), 0.02),
        'even_w_in': nrm((N_EVEN, D, EVEN_IN), D ** -0.5),
        'even_w_out': nrm((N_EVEN, A_WIDTH + B_WIDTH, D), (A_WIDTH + B_WIDTH) ** -0.5),
        'ssm_lambda_re': -0.5 + nrm((N_EVEN, G, P), 0.01),
        'ssm_lambda_im': jnp.pi * jnp.arange(P, dtype=f32) + nrm((N_EVEN, G, P), 0.01),
        'ssm_log_step': unif((N_EVEN, G), math.log(S5_MIN_STEP), math.log(S5_MAX_STEP)),
        'ssm_b_re': nrm((N_EVEN, G, P, H), (2 * H) ** -0.5),
        'ssm_b_im': nrm((N_EVEN, G, P, H), (2 * H) ** -0.5),
        'ssm_c_re': nrm((N_EVEN, G, H, P), P ** -0.5),
        'ssm_c_im': nrm((N_EVEN, G, H, P), P ** -0.5),
        'ssm_d': nrm((N_EVEN, G, H)),
        'ssm_glu_w': nrm((N_EVEN, A_WIDTH, A_WIDTH), A_WIDTH ** -0.5),
        'ssm_glu_b': nrm((N_EVEN, A_WIDTH), 0.01),
        'hgrn_lower_bounds': nrm((N_EVEN, B_WIDTH), 0.1),
        'hgrn_norm_w': 1.0 + nrm((N_EVEN, B_VDIM), 0.02),
        'rw_mix': unif((N_ODD, 6, D), 0.0, 1.0),
        'rw_w_rkvz': nrm((N_ODD, 4, D, D), D ** -0.5),
        'rw_w0': unif((N_ODD, D), -6.0, 0.0),
        'rw_w1': nrm((N_ODD, D, C_DECAY_LORA), D ** -0.5),
        'rw_w2': nrm((N_ODD, C_DECAY_LORA, D), 0.1 * C_DECAY_LORA ** -0.5),
        'rw_a0': nrm((N_ODD, D), 0.1),
        'rw_a1': nrm((N_ODD, D, C_AAA_LORA), D ** -0.5),
        'rw_a2': nrm((N_ODD, C_AAA_LORA, D), 0.1 * C_AAA_LORA ** -0.5),
        'rw_v0': 1.0 + nrm((C_VRES_LAYERS, D), 0.1),
        'rw_v1': nrm((C_VRES_LAYERS, D, C_MV_LORA), D ** -0.5),
        'rw_v2': nrm((C_VRES_LAYERS, C_MV_LORA, D), 0.1 * C_MV_LORA ** -0.5),
        'rw_g1': nrm((N_ODD, D, C_GATE_LORA), D ** -0.5),
        'rw_g2': nrm((N_ODD, C_GATE_LORA, D), C_GATE_LORA ** -0.5),
        'rw_k_k': 0.85 + nrm((N_ODD, D), 0.02),
        'rw_k_a': 1.0 + nrm((N_ODD, D), 0.02),
        'rw_r_k': nrm((N_ODD, C_HEADS, C_HEAD), 0.1),
        'rw_ln_w': 1.0 + nrm((N_ODD, D), 0.02),
        'rw_ln_b': nrm((N_ODD, D), 0.01),
        'rw_w_o': nrm((N_ODD, D, D), D ** -0.5),
    }


def reference(x_prompt, x_sample, state_ssm_re, state_ssm_im, state_hgrn, state_wkv, state_shift,
              norm_w, final_norm_w, even_w_in, even_w_out, ssm_lambda_re, ssm_lambda_im, ssm_log_step,
              ssm_b_re, ssm_b_im, ssm_c_re, ssm_c_im, ssm_d, ssm_glu_w, ssm_glu_b,
              hgrn_lower_bounds, hgrn_norm_w, rw_mix, rw_w_rkvz, rw_w0, rw_w1, rw_w2,
              rw_a0, rw_a1, rw_a2, rw_v0, rw_v1, rw_v2, rw_g1, rw_g2, rw_k_k, rw_k_a, rw_r_k,
              rw_ln_w, rw_ln_b, rw_w_o):
    w = dict(norm_w=norm_w, final_norm_w=final_norm_w, even_w_in=even_w_in, even_w_out=even_w_out,
             ssm_lambda_re=ssm_lambda_re, ssm_lambda_im=ssm_lambda_im, ssm_log_step=ssm_log_step,
             ssm_b_re=ssm_b_re, ssm_b_im=ssm_b_im, ssm_c_re=ssm_c_re, ssm_c_im=ssm_c_im, ssm_d=ssm_d,
             ssm_glu_w=ssm_glu_w, ssm_glu_b=ssm_glu_b, hgrn_lower_bounds=hgrn_lower_bounds,
             hgrn_norm_w=hgrn_norm_w, rw_mix=rw_mix, rw_w_rkvz=rw_w_rkvz, rw_w0=rw_w0, rw_w1=rw_w1,
             rw_w2=rw_w2, rw_a0=rw_a0, rw_a1=rw_a1, rw_a2=rw_a2, rw_v0=rw_v0, rw_v1=rw_v1, rw_v2=rw_v2,
             rw_g1=rw_g1, rw_g2=rw_g2, rw_k_k=rw_k_k, rw_k_a=rw_k_a, rw_r_k=rw_r_k,
             rw_ln_w=rw_ln_w, rw_ln_b=rw_ln_b, rw_w_o=rw_w_o)
    f32 = jnp.float32
    bp = x_prompt.shape[0]
    z_re = jnp.zeros((N_EVEN, bp, A_GROUPS, A_STATE), f32)
    z_hg = jnp.zeros((N_EVEN, bp, B_HEADS, B_KDIM, B_VDIM), f32)
    z_wkv = jnp.zeros((N_ODD, bp, C_HEADS, C_HEAD, C_HEAD), f32)
    z_sh = jnp.zeros((N_ODD, bp, D_MODEL), f32)
    y_prompt, p_re, p_im, p_hg, p_wkv, p_sh = _trunk(x_prompt, z_re, z_re, z_hg, z_wkv, z_sh, w)
    y_sample, s_re, s_im, s_hg, s_wkv, s_sh = _trunk(
        x_sample, state_ssm_re, state_ssm_im, state_hgrn, state_wkv, state_shift, w)
    return (y_prompt, y_sample, p_re, p_im, p_hg, p_wkv, p_sh, s_re, s_im, s_hg, s_wkv, s_sh)
```

```python
import contextlib
import math
import numpy as np
import concourse.bass as bass
import concourse.mybir as mybir
from concourse.bass_utils import run_bass_kernel_spmd

F32 = mybir.dt.float32
BF16 = mybir.dt.bfloat16
ALU = mybir.AluOpType
AF = mybir.ActivationFunctionType

NCORES = 8
DBG = {'stage': 9, 'tiles': 99, 'samp': True}
D = 1024
SEQ = 2048
NS = 16
ST = 4
NTOK = SEQ + NS * ST
NTILE = SEQ // 128 + 1
MAGIC = float(1.5 * 2 ** 23)
TWO_PI = float(2 * math.pi)
RMS_EPS = 1e-6
GN_EPS = 64e-5
DECAY_SCALE = math.exp(-0.5)


class V:
    __slots__ = ('ap', 'keys')

    def __init__(self, ap, keys):
        self.ap = ap
        self.keys = keys

    def __getitem__(self, idx):
        return V(self.ap[idx], self.keys)

    def re(self, pat, **kw):
        return V(self.ap.rearrange(pat, **kw), self.keys)

    def bc(self, shape):
        return V(self.ap.to_broadcast(list(shape)), self.keys)

    def us(self, axis):
        return V(self.ap.unsqueeze(axis), self.keys)


class B:
    def __init__(self, t, key):
        self.t = t
        self.key = key

    def __getitem__(self, idx):
        return V(self.t[idx], (self.key,))

    def k(self, sub, idx=slice(None)):
        return V(self.t[idx], ((self.key, sub),))


class Prog:
    ENGS = ('pe', 'act', 'dve', 'pool', 'sp')
    EPOCH = 4000
    NDSEM = 8

    def __init__(self, nc):
        self.nc = nc
        self.q = {e: [] for e in self.ENGS}
        self.cnt = {e: 0 for e in self.ENGS}
        self.seen = {e: {} for e in self.ENGS}
        self.reg = {}
        self.sems = {}
        self.dma_i = 0
        self.dma_val = [0] * self.NDSEM
        self.last = {}

    def _r(self, key):
        r = self.reg.get(key)
        if r is None:
            r = [None, {}]
            self.reg[key] = r
        return r

    def _need(self, eng, tok, waits):
        if tok is None:
            return
        sk, v = tok
        if self.seen[eng].get(sk, 0) >= v:
            return
        if v > waits.get(sk, 0):
            waits[sk] = v

    def op(self, eng, emit, reads=(), writes=(), dma=False):
        waits = {}
        for k in reads:
            r = self._r(k)
            self._need(eng, r[0], waits)
            if isinstance(k, str) and k.startswith('ps'):
                for tok in r[1].values():
                    if tok[0][0] != eng:
                        self._need(eng, tok, waits)
        for k in writes:
            r = self._r(k)
            w = r[0]
            if not (eng == 'pe' and w is not None and w[0][0] == 'pe'):
                self._need(eng, w, waits)
            for tok in r[1].values():
                self._need(eng, tok, waits)
        if dma:
            di = self.dma_i % self.NDSEM
            self.dma_i += 1
            sk = ('dma', di)
            prev = self.dma_val[di]
            if prev > 0:
                self._need(eng, (sk, prev), waits)
            self.dma_val[di] = prev + 16
            tok = (sk, prev + 16)
            inc = 16
        else:
            c = self.cnt[eng]
            sk = (eng, c // self.EPOCH)
            tok = (sk, c % self.EPOCH + 1)
            self.cnt[eng] = c + 1
            inc = 1
        for sk2, v in waits.items():
            self.seen[eng][sk2] = v
        self.q[eng].append((emit, list(waits.items()), sk, inc))
        self.last[sk] = tok
        for k in reads:
            self._r(k)[1][tok[0]] = tok
        for k in writes:
            r = self._r(k)
            r[0] = tok
            r[1] = {}
        return tok

    def barrier(self, engs=None):
        toks = list(self.last.values())
        for e in (engs or self.ENGS):
            waits = {}
            for t in toks:
                self._need(e, t, waits)
            for sk2, v in waits.items():
                self.seen[e][sk2] = v
            if waits:
                self.q[e].append((None, list(waits.items()), None, 0))

    def emit_all(self, stack):
        nc = self.nc
        for e in self.ENGS:
            for (emit, waits, sk, inc) in self.q[e]:
                for s in ([sk] if sk is not None else []) + [w[0] for w in waits]:
                    if s not in self.sems:
                        self.sems[s] = stack.enter_context(
                            nc.semaphore("s_" + "_".join(str(x) for x in s)))
        block = stack.enter_context(nc.Block())
        sems = self.sems

        def run(e):
            def body(h):
                for (emit, waits, sk, inc) in self.q[e]:
                    for sk2, v in waits:
                        h.wait_ge(sems[sk2], v)
                    if emit is not None:
                        emit(h).then_inc(sems[sk], inc)
            return body

        block.tensor(run('pe'))
        block.scalar(run('act'))
        block.vector(run('dve'))
        block.gpsimd(run('pool'))
        block.sync(run('sp'))


def _ks(*vs):
    out = []
    for v in vs:
        if isinstance(v, V):
            out.extend(v.keys)
    return out


def _a(x):
    return x.ap if isinstance(x, V) else x


class Ops:
    def __init__(self, P):
        self.P = P
        self.rr = 0

    def mm(self, out, lhsT, rhs, start=True, stop=True):
        self.P.op('pe', lambda e: e.matmul(out.ap, lhsT.ap, rhs.ap, start=start, stop=stop),
                  reads=_ks(lhsT, rhs), writes=_ks(out))

    def tr(self, out, in_, ident):
        self.P.op('pe', lambda e: e.transpose(out.ap, in_.ap, ident.ap), reads=_ks(in_, ident), writes=_ks(out))

    def act(self, out, in_, func, bias=0.0, scale=1.0, accum=None):
        kw = {}
        if accum is not None:
            kw['accum_out'] = accum.ap
        self.P.op('act', lambda e: e.activation(out.ap, in_.ap, func, bias=_a(bias), scale=_a(scale), **kw),
                  reads=_ks(in_, bias, scale), writes=_ks(out, accum))

    def tt(self, eng, out, a, b, op):
        self.P.op(eng, lambda e: e.tensor_tensor(out.ap, a.ap, b.ap, op), reads=_ks(a, b), writes=_ks(out))

    def ts(self, eng, out, a, s1, op0, s2=None, op1=None):
        if op1 is None:
            self.P.op(eng, lambda e: e.tensor_scalar(out.ap, a.ap, _a(s1), None, op0),
                      reads=_ks(a, s1), writes=_ks(out))
        else:
            self.P.op(eng, lambda e: e.tensor_scalar(out.ap, a.ap, _a(s1), _a(s2), op0, op1),
                      reads=_ks(a, s1, s2), writes=_ks(out))

    def stt(self, eng, out, a, s, b, op0, op1):
        self.P.op(eng, lambda e: e.scalar_tensor_tensor(out.ap, a.ap, _a(s), b.ap, op0, op1),
                  reads=_ks(a, s, b), writes=_ks(out))

    def scan(self, out, d0, d1, init, op0=ALU.mult, op1=ALU.add):
        self.P.op('dve', lambda e: e.tensor_tensor_scan(out.ap, d0.ap, d1.ap, _a(init), op0, op1),
                  reads=_ks(d0, d1, init), writes=_ks(out))

    def cp(self, eng, out, in_):
        if eng == 'act':
            self.P.op('act', lambda e: e.activation(out.ap, in_.ap, AF.Copy), reads=_ks(in_), writes=_ks(out))
        else:
            self.P.op(eng, lambda e: e.tensor_copy(out.ap, in_.ap), reads=_ks(in_), writes=_ks(out))

    def recip(self, out, in_):
        self.P.op('dve', lambda e: e.reciprocal(out.ap, in_.ap), reads=_ks(in_), writes=_ks(out))

    def memset(self, eng, out, val):
        self.P.op(eng, lambda e: e.memset(out.ap, val), writes=_ks(out))

    def dma(self, out, in_, eng='sp', rk=(), wk=()):
        return self.P.op(eng, lambda e: e.dma_start(out=_a(out), in_=_a(in_)),
                         reads=_ks(in_) + list(rk), writes=_ks(out) + list(wk), dma=True)

    def any_eng(self, choices=('dve', 'pool')):
        self.rr += 1
        return choices[self.rr % len(choices)]


def _consts():
    c = {}
    c['ident'] = np.eye(128, dtype=np.float32)
    col = np.arange(512)
    c['m16scan'] = np.broadcast_to((col % 16 != 0).astype(np.float32), (128, 512)).copy()
    col = np.arange(256)
    c['m4scan'] = np.broadcast_to((col % 4 != 0).astype(np.float32), (128, 256)).copy()
    s = np.arange(128)[:, None]
    t = np.arange(128)[None, :]
    c['att16'] = ((s // 16 == t // 16) & (s <= t)).astype(np.float32)
    c['att4'] = ((s // 4 == t // 4) & (s <= t)).astype(np.float32)
    c['cm16'] = (s // 16 == np.arange(8)[None, :]).astype(np.float32)
    c['cm4'] = (s // 4 == np.arange(16)[None, :]).astype(np.float32)
    c['iota1'] = np.broadcast_to((np.arange(128) + 1).astype(np.float32), (128, 128)).copy()
    for nm, C in (('p', 64), ('s', 4)):
        idx = np.arange(128)
        blk = idx // C
        pos = idx % C
        same = blk[:, None] == blk[None, :]
        c['su_' + nm] = (same & (pos[:, None] < pos[None, :])).astype(np.float32)
        c['iu_' + nm] = (same & (pos[:, None] <= pos[None, :])).astype(np.float32)
        c['sl_' + nm] = (same & (pos[:, None] > pos[None, :])).astype(np.float32)
    c['seqsel'] = (np.arange(128)[:, None] // 8 == np.arange(16)[None, :]).astype(np.float32)
    f = np.arange(128)
    c['blk64'] = (f[:, None] // 64 == f[None, :] // 64).astype(np.float32) / 64.0
    c['blk64u'] = (f[:, None] // 64 == f[None, :] // 64).astype(np.float32)
    c['m64scan'] = np.broadcast_to((np.arange(128) % 64 != 0).astype(np.float32), (128, 128)).copy()
    c['am_p'] = np.concatenate([c['su_p'], c['iu_p'], c['su_p'], c['iu_p']], axis=1)
    c['am_s'] = np.concatenate([c['su_s'], c['iu_s'], c['su_s'], c['iu_s']], axis=1)
    for k in ('su_p', 'iu_p', 'su_s', 'iu_s', 'seqsel_unused'):
        c.pop(k, None)
    offs = {}
    o = 0
    for k, v in c.items():
        offs[k] = (o, v.shape[1])
        o += v.shape[1]
    arr = np.concatenate(list(c.values()), axis=1).astype(np.float32)
    return arr, offs


CONST_ARR, CONST_OFF = _consts()


def build_nc(n_layers=4):
    nc = bass.Bass("TRN2", target_bir_lowering=False)

    def din(name, shape):
        return nc.dram_tensor(name, list(shape), F32, kind="ExternalInput").ap()

    def dout(name, shape):
        return nc.dram_tensor(name, list(shape), F32, kind="ExternalOutput").ap()

    x_in = din("x", [NTOK, D])
    st_re = din("st_re", [2, NS, 2048])
    st_im = din("st_im", [2, NS, 2048])
    st_hg = din("st_hg", [2, NS, 4, 128, 128])
    st_wkv = din("st_wkv", [2, NS, 16, 64, 64])
    st_sh = din("st_sh", [2, NS, D])
    consts_d = din("consts", list(CONST_ARR.shape))
    norm_w = din("norm_w", [4, D])
    fnorm_w = din("final_norm_w", [1, D])
    w_in = din("even_w_in", [2, D, 3072])
    w_out = din("even_w_out", [2, D, D])
    lam_pp = din("lam_pp", [2, 3, 128, 16])
    lam_row = din("lam_row", [2, 3, 2048])
    bpad = din("bpad", [2, 2, 128, 2048])
    cpad = din("cpad", [2, 2, 128, 2048])
    ssm_d = din("ssm_d_l", [2, 128, 4])
    glu_w = din("ssm_glu_w", [2, 512, 512])
    glu_b = din("glu_b_l", [2, 128, 4])
    hlb = din("hlb_l", [2, 128, 4])
    hnw = din("hnw_l", [2, 128, 1])

    w_rkvz = din("rw_w_rkvz", [2, 4, D, D])
    w_o = din("rw_w_o", [2, D, D])
    lora1 = din("lora1", [2, D, 320])
    lw2 = din("lw2", [2, 64, D])
    la2 = din("la2", [2, 64, D])
    lv2 = din("lv2", [2, 32, D])
    lg2 = din("lg2", [2, 160, D])
    prm_odd = din("prm_odd", [2, 128, 14, 8])

    y_out = dout("y", [NTOK, D])
    p_re = dout("p_re", [2, 16, 128])
    p_im = dout("p_im", [2, 16, 128])
    p_hg = dout("p_hg", [2, 4, 128, 128])
    p_wkv = dout("p_wkv", [2, 16, 64, 64])
    p_sh = dout("p_sh", [2, D])
    s_re = dout("s_re", [2, NS, 2048])
    s_im = dout("s_im", [2, NS, 2048])
    s_hg = dout("s_hg", [2, NS, 4, 128, 128])
    s_wkv = dout("s_wkv", [2, NS, 16, 64, 64])
    s_sh = dout("s_sh", [2, NS, D])

    r_a = nc.dram_tensor("r_a", [NTOK, D], F32).ap()
    r_b = nc.dram_tensor("r_b", [NTOK, D], F32).ap()
    vf_d = nc.dram_tensor("vf_d", [NTILE, 128, D], F32).ap()

    with contextlib.ExitStack() as st:
        P = Prog(nc)
        O = Ops(P)

        def sb(name, shape, dt=F32, stack=st):
            return B(stack.enter_context(nc.sbuf_tensor(name, list(shape), dt)), name)

        ps = [B(st.enter_context(nc.psum_tensor("ps%d" % i, [128, 512], F32)), "ps%d" % i) for i in range(8)]

        NC_ = CONST_ARR.shape[1]
        cst = sb("cst", [128, NC_])
        O.dma(cst[:, :], consts_d[:, :])

        def C(name, rows=128):
            o, w = CONST_OFF[name]
            return cst[0:rows, o:o + w]

        ident = C('ident')
        ones_bf = sb("ones_bf", [128, 128], BF16)
        O.memset('pool', ones_bf[:, :], 1.0 / 128)
        cstb = sb("cstb", [128, 384], BF16)
        O.cp('pool', cstb[:, 256:384], C('ident'))
        O.cp('dve', cstb[:, 0:128], C('blk64u'))
        O.cp('dve', cstb[:, 128:256], C('blk64'))
        xt = sb("xt", [128, D])
        xn = sb("xn", [128, D])
        nwb = sb("nwb", [128, D])
        fnwb = sb("fnwb", [128, D])
        O.dma(fnwb[:, :], fnorm_w[0:1, :].broadcast_to([128, D]))
        small = sb("small", [128, 8])
        outT = sb("outT", [128, 8, 128], BF16)
        xnT_box = [None]
        out_toks = []

        srcs = [x_in, r_a, r_b, r_a, r_b]
        srck = ['x', 'ra', 'rb', 'ra', 'rb']

        def tile_info(ti):
            if ti < NTILE - 1:
                return ti * 128, 128
            return SEQ, NS * ST

        def load_and_norm(layer, ti, want_f32T=None):
            tok0, TT = tile_info(ti)
            src = srcs[layer]
            O.dma(xt[0:TT, :], src[tok0:tok0 + TT, :], rk=[(srck[layer], ti)] if layer > 0 else [])
            O.act(xn[0:TT, :], xt[0:TT, :], AF.Square, accum=small[0:TT, 0:1])
            O.act(small[0:TT, 1:2], small[0:TT, 0:1], AF.Sqrt, bias=RMS_EPS, scale=1.0 / D)
            O.recip(small[0:TT, 1:2], small[0:TT, 1:2])
            O.stt('dve', xn[0:TT, :], xt[0:TT, :], small[0:TT, 1:2], nwb[0:TT, :], ALU.mult, ALU.mult)
            for half in range(2):
                bank = ps[half]
                for c4 in range(4):
                    c = half * 4 + c4
                    O.tr(bank[:, c4 * 128:c4 * 128 + TT], xn[0:TT, c * 128:(c + 1) * 128], ident[0:TT, 0:TT])
                src_v = bank[:, :].re("p (c t) -> p c t", c=4)[:, :, 0:TT]
                if want_f32T is not None:
                    O.cp('act' if half == 0 else 'dve', want_f32T[:, half * 4:half * 4 + 4, 0:TT], src_v)
                else:
                    O.cp('act' if half == 0 else 'dve', xnT_box[0][:, half * 4:half * 4 + 4, 0:TT], src_v)

        def out_proj_and_store(layer, ti, Wo):
            tok0, TT = tile_info(ti)
            last = (layer == n_layers - 1)
            for half in range(2):
                bank = ps[4 + half]
                for kc in range(8):
                    O.mm(bank[0:TT, :], outT[:, kc, 0:TT], Wo(kc, half),
                         start=(kc == 0), stop=(kc == 7))
                O.tt('dve', xt[0:TT, half * 512:(half + 1) * 512], bank[0:TT, :],
                     xt[0:TT, half * 512:(half + 1) * 512], ALU.add)
            if not last:
                dst = srcs[layer + 1]
                O.dma(dst[tok0:tok0 + TT, :], xt[0:TT, :], wk=[(srck[layer + 1], ti)])
            else:
                O.act(xn[0:TT, :], xt[0:TT, :], AF.Square, accum=small[0:TT, 2:3])
                O.act(small[0:TT, 3:4], small[0:TT, 2:3], AF.Sqrt, bias=RMS_EPS, scale=1.0 / D)
                O.recip(small[0:TT, 3:4], small[0:TT, 3:4])
                O.stt('dve', xn[0:TT, :], xt[0:TT, :], small[0:TT, 3:4], fnwb[0:TT, :], ALU.mult, ALU.mult)
                out_toks.append(O.dma(y_out[tok0:tok0 + TT, :], xn[0:TT, :]))

        def even_layer(layer):
            j = layer // 2
            with contextlib.ExitStack() as ls:
                def lsb(name, shape, dt=F32):
                    return sb("e%d_%s" % (layer, name), shape, dt, stack=ls)

                xnT = lsb("xnT", [128, 8, 128], BF16)
                xnT_box[0] = xnT
                Win = lsb("Win", [128, 8, 3072], BF16)
                Wout = lsb("Wout", [128, 8, 1024], BF16)
                Wglu = lsb("Wglu", [128, 4, 512], BF16)
                Bp = lsb("Bp", [128, 2, 2048], BF16)
                Cp = lsb("Cp", [128, 2, 2048], BF16)
                Dd = lsb("Dd", [128, 4, 128], BF16)
                prm = lsb("prm", [128, 16])
                hn = lsb("hn", [128, 2])
                Etab = lsb("Etab", [128, 2, 16, 128])
                pp = lsb("pp", [128, 10, 16])
                rho_s = lsb("rho_s", [128, 16, 64])
                hc = lsb("hc", [128, 2, 16])
                h0 = lsb("h0", [128, 2, 16, 16])
                hs = lsb("hs", [128, 2, 16, 16])
                A = lsb("A", [128, 19, 512])
                u_bf = lsb("u_bf", [128, 4, 128], BF16)
                sza = lsb("sza", [128, 4, 128])
                sq = lsb("sq", [128, 4, 128])
                sf = lsb("sf", [128, 4, 128])
                szb = lsb("szb", [128, 4, 128])
                i_tm = lsb("i_tm", [128, 512], BF16)
                hb = lsb("hb", [128, 2, 4, 128], BF16)
                gl_bf = lsb("gl_bf", [128, 4, 128], BF16)
                osq = lsb("osq", [128, 4, 128], BF16)
                qinb = lsb("qinb", [128, 4, 128], BF16)
                kinb = lsb("kinb", [128, 4, 128], BF16)
                attT = lsb("attT", [128, 4, 128], BF16)
                kexp = lsb("kexp", [128, 16 * 128], BF16)
                Sst = lsb("Sst", [128, 4, 2, 128])
                dec = lsb("dec", [128, 64])

                def At(i, w=512):
                    return V(A.t[:, i, 0:w], (("A", layer, i),))

                def Ag(i, n):
                    return V(A.t[:, i:i + n, :], tuple(("A", layer, i + q) for q in range(n)))

                O.dma(nwb[:, :], norm_w[layer:layer + 1, :].broadcast_to([128, D]))
                stg = [xt[:, :], xn[:, :]] + [Ag(2 * q_, 2).re("p a b -> p (a b)") for q_ in range(4)]
                si = [0]

                def load_cast(dst_v, src_ap, scale=None):
                    s = stg[si[0] % len(stg)]
                    si[0] += 1
                    w = dst_v.ap.shape[-1] if len(dst_v.ap.shape) == 2 else None
                    rows, cols = src_ap.shape
                    O.dma(s[0:rows, 0:cols], src_ap, eng=('sp', 'act')[si[0] % 2])
                    eng = ('act', 'dve', 'pool')[si[0] % 3]
                    if scale is None:
                        O.cp(eng, dst_v, s[0:rows, 0:cols])
                    else:
                        O.ts('dve' if eng == 'act' else eng, dst_v, s[0:rows, 0:cols], scale, ALU.mult)

                for kc in range(8):
                    for q in range(3):
                        load_cast(Win.k(kc, (slice(None), kc, slice(q * 1024, (q + 1) * 1024))),
                                  w_in[j, kc * 128:(kc + 1) * 128, q * 1024:(q + 1) * 1024])
                for kc in range(8):
                    load_cast(Wout.k(kc, (slice(None), kc, slice(None))), w_out[j, kc * 128:(kc + 1) * 128, :])
                for kc in range(4):
                    load_cast(Wglu.k(kc, (slice(None), kc, slice(None))), glu_w[j, kc * 128:(kc + 1) * 128, :])
                for ri in range(2):
                    for hf in range(2):
                        load_cast(Cp.k((ri, hf), (slice(None), ri, slice(hf * 1024, (hf + 1) * 1024))),
                                  cpad[j, ri, :, hf * 1024:(hf + 1) * 1024], scale=(-1.0 if ri == 1 else None))
                O.dma(prm[:, 0:4], ssm_d[j])
                O.dma(prm[:, 4:8], glu_b[j])
                O.dma(prm[:, 8:12], hlb[j])
                O.dma(prm[:, 12:16], hlb[0])
                O.dma(hn[:, 0:1], hnw[j])
                if j == 0:
                    O.memset('dve', prm[:, 8:12], 0.0)
                    O.memset('dve', prm[:, 12:16], 1.0)
                else:
                    O.tt('dve', prm[:, 8:12], prm[:, 8:12], prm[:, 12:16], ALU.subtract)
                    O.act(prm[:, 8:12], prm[:, 8:12], AF.Sigmoid)
                    O.ts('dve', prm[:, 12:16], prm[:, 8:12], -1.0, ALU.mult, 1.0, ALU.add)
                for oc in range(4):
                    O.ts('dve', Dd[:, oc, :], ident, prm[:, oc:oc + 1], ALU.mult)

                def s5_params(lre, lim, lst, T):
                    a1, t1, t2, t3, t4, t5, t6 = T
                    lr = lre
                    O.ts('dve', lr, lre, -1e-4, ALU.min)
                    O.act(lst, lst, AF.Exp)
                    O.tt('dve', a1, lr, lst, ALU.mult)
                    O.act(a1, a1, AF.Exp)
                    th = lst
                    O.tt('dve', th, lim, lst, ALU.mult)
                    O.ts('dve', t1, th, 1.0 / TWO_PI, ALU.mult, MAGIC, ALU.add)
                    O.ts('dve', t1, t1, MAGIC, ALU.subtract)
                    O.stt('dve', th, t1, -TWO_PI, th, ALU.mult, ALU.add)
                    O.act(t2, th, AF.Sin)
                    O.stt('dve', t1, th, -1.0, th, ALU.mult, ALU.max)
                    O.act(t1, t1, AF.Sin, bias=float(math.pi / 2), scale=-1.0)
                    O.tt('dve', t1, t1, a1, ALU.mult)
                    O.tt('dve', t2, t2, a1, ALU.mult)
                    O.tt('dve', t3, lr, lr, ALU.mult)
                    O.tt('dve', t4, lim, lim, ALU.mult)
                    O.tt('dve', t3, t3, t4, ALU.add)
                    O.recip(t3, t3)
                    O.ts('dve', t4, t1, -1.0, ALU.add)
                    O.tt('dve', t5, t4, lr, ALU.mult)
                    O.tt('dve', t6, t2, lim, ALU.mult)
                    O.tt('dve', t5, t5, t6, ALU.add)
                    O.tt('dve', t5, t5, t3, ALU.mult)
                    O.tt('dve', t6, t2, lr, ALU.mult)
                    O.tt('dve', t4, t4, lim, ALU.mult)
                    O.tt('dve', t6, t6, t4, ALU.subtract)
                    O.tt('dve', t6, t6, t3, ALU.mult)
                    return dict(rho=a1, th=th, cr=t5, ci=t6)

                for q in range(3):
                    O.dma(pp[:, q, :], lam_pp[j, q])
                r1 = s5_params(pp[:, 0, :], pp[:, 1, :], pp[:, 2, :], [pp[:, 3 + q, :] for q in range(7)])
                rho = r1['rho']
                th = r1['th']
                ang = Ag(0, 4).re("p a (g t) -> p (a g) t", g=4)
                kf2 = Ag(4, 4).re("p a (g t) -> p (a g) t", g=4)
                O.tt('dve', ang, C('iota1').us(1).bc([128, 16, 128]), th.us(2).bc([128, 16, 128]), ALU.mult)
                O.ts('dve', kf2, ang, 1.0 / TWO_PI, ALU.mult, MAGIC, ALU.add)
                O.ts('dve', kf2, kf2, MAGIC, ALU.subtract)
                O.stt('dve', ang, kf2, -TWO_PI, ang, ALU.mult, ALU.add)
                O.act(Etab[:, 1, :, :], ang, AF.Sin)
                O.stt('dve', ang, ang, -1.0, ang, ALU.mult, ALU.max)
                O.act(Etab[:, 0, :, :], ang, AF.Sin, bias=float(math.pi / 2), scale=-1.0)
                O.tt('dve', rho_s[:, :, :], rho.us(2).bc([128, 16, 64]),
                     C('m4scan')[:, 0:64].us(1).bc([128, 16, 64]), ALU.mult)
                O.memset('pool', hc[:, :, :], 0.0)
                for qq in range(4):
                    cs_ = slice(qq * 512, (qq + 1) * 512)
                    for q in range(3):
                        O.dma(At(q), lam_row[j, q:q + 1, cs_].broadcast_to([128, 512]))
                    r2 = s5_params(At(0), At(1), At(2), [At(3 + q) for q in range(7)])
                    cr, ci = r2['cr'], r2['ci']
                    bre = xt[:, 0:512]
                    bim = xt[:, 512:1024]
                    O.dma(bre, bpad[j, 0, :, cs_])
                    O.dma(bim, bpad[j, 1, :, cs_])
                    t1 = xn[:, 0:512]
                    t2 = xn[:, 512:1024]
                    O.tt('dve', t1, bre, cr, ALU.mult)
                    O.tt('pool', t2, bim, ci, ALU.mult)
                    O.tt('dve', Bp.k((0, qq), (slice(None), 0, cs_)), t1, t2, ALU.subtract)
                    O.tt('dve', t1, bim, cr, ALU.mult)
                    O.tt('pool', t2, bre, ci, ALU.mult)
                    O.tt('dve', Bp.k((1, qq), (slice(None), 1, cs_)), t1, t2, ALU.add)
                Bk = lambda ri, gp: Bp.k((ri, gp // 4), (slice(None), ri, slice(gp * 128, (gp + 1) * 128)))
                Ck = lambda ri, gp: Cp.k((ri, gp // 8), (slice(None), ri, slice(gp * 128, (gp + 1) * 128)))

                for ri, srcst in enumerate((st_re, st_im)):
                    for hf in range(2):
                        stage = xt if hf == 0 else xn
                        O.dma(stage[0:NS, :], srcst[j, :, hf * 1024:(hf + 1) * 1024])
                        bank = ps[2 + hf]
                        for g8 in range(8):
                            O.tr(bank[:, g8 * 16:(g8 + 1) * 16], stage[0:NS, g8 * 128:(g8 + 1) * 128], ident[0:NS, 0:NS])
                        O.cp('dve', h0[:, ri, hf * 8:(hf + 1) * 8, :],
                             bank[:, 0:128].re("p (g s) -> p g s", g=8))
                for hd in range(4):
                    for sl in range(2):
                        O.memset('pool', Sst.k((hd, sl), (slice(None), hd, sl, slice(None))), 0.0)

                for ti in range(NTILE):
                    tok0, TT = tile_info(ti)
                    samp = (ti == NTILE - 1)
                    CH = 4 if samp else 16
                    NCH = TT // CH
                    load_and_norm(layer, ti)
                    grp = [(0, ps[2]), (4, ps[3]), (8, ps[4]), (12, ps[5]), (20, ps[6])]
                    for (oc0, bank) in grp:
                        for o4 in range(4):
                            oc = oc0 + o4
                            for kc in range(8):
                                O.mm(bank[:, o4 * 128:o4 * 128 + TT],
                                     Win.k(kc, (slice(None), kc, slice(oc * 128, (oc + 1) * 128))),
                                     xnT[:, kc, 0:TT], start=(kc == 0), stop=(kc == 7))
                    for kc in range(8):
                        O.mm(ps[7][0:TT, :], xnT[:, kc, 0:TT],
                             Win.k(kc, (slice(None), kc, slice(16 * 128, 20 * 128))), start=(kc == 0), stop=(kc == 7))

                    def pv(bank):
                        return bank[:, :].re("p (c t) -> p c t", c=4)[:, :, 0:TT]

                    def Tf(i):
                        return At(i)[:, 0:4 * TT]

                    def T4(i):
                        return At(i)[:, 0:4 * TT].re("p (c t) -> p c t", c=4)

                    O.cp('act', u_bf[:, :, 0:TT], pv(ps[2]))
                    O.act(sza[:, :, 0:TT], pv(ps[3]), AF.Silu)
                    O.act(sq[:, :, 0:TT], pv(ps[4]), AF.Silu)
                    O.act(sf[:, :, 0:TT], pv(ps[5]), AF.Sigmoid)
                    O.act(szb[:, :, 0:TT], pv(ps[6]), AF.Silu)
                    O.cp('dve', i_tm[0:TT, :], ps[7][0:TT, :])

                    def chainA():
                        ybank = ps[2]
                        for q in range(4):
                            bre, bim = ps[0], ps[1]
                            for i4 in range(4):
                                gp = q * 4 + i4
                                O.mm(bre[:, i4 * 128:i4 * 128 + TT], Bk(0, gp), u_bf[:, q, 0:TT])
                                O.mm(bim[:, i4 * 128:i4 * 128 + TT], Bk(1, gp), u_bf[:, q, 0:TT])

                            def rot(eng, out, a, ri, q=q):
                                if not samp:
                                    O.tt(eng, out, a, Etab[:, ri, q * 4:q * 4 + 4, 0:TT], ALU.mult)
                                else:
                                    for i4 in range(4):
                                        O.tt(eng, out[:, i4, :].re("p (s t) -> p s t", t=ST),
                                             a[:, i4, :].re("p (s t) -> p s t", t=ST),
                                             Etab[:, ri, q * 4 + i4, 0:ST].us(1).bc([128, NS, ST]), ALU.mult)
                            yield
                            rot('dve', T4(0), pv(bre), 0)
                            rot('dve', T4(1), pv(bim), 1)
                            O.tt('pool', T4(4), T4(0), T4(1), ALU.add)
                            rot('dve', T4(2), pv(bim), 0)
                            rot('dve', T4(3), pv(bre), 1)
                            O.tt('pool', T4(5), T4(2), T4(3), ALU.subtract)
                            if samp:
                                for ri, gt in ((0, 4), (1, 5)):
                                    tmp = At(0)[:, 0:64].re("p (g s) -> p g s", g=4)
                                    O.tt('dve', tmp, h0[:, ri, q * 4:q * 4 + 4, :],
                                         rho[:, q * 4:q * 4 + 4].us(2).bc([128, 4, NS]), ALU.mult)
                                    gv = Tf(gt).re("p (g s t) -> p g s t", g=4, t=ST)[:, :, :, 0]
                                    O.tt('dve', gv, gv, tmp, ALU.add)
                            yield
                            for i4 in range(4):
                                gp = q * 4 + i4
                                for ri, (gt, go) in enumerate(((4, 6), (5, 7))):
                                    if samp:
                                        O.scan(T4(go)[:, i4, :], rho_s[:, gp, :], T4(gt)[:, i4, :], 0.0)
                                    else:
                                        O.scan(T4(go)[:, i4, :], rho[:, gp:gp + 1].bc([128, TT]), T4(gt)[:, i4, :],
                                               hc[:, ri, gp:gp + 1])
                            yield
                            rot('pool', T4(0), T4(6), 0)
                            rot('dve', T4(1), T4(7), 1)
                            O.tt('pool', hb[:, 0, :, 0:TT], T4(0), T4(1), ALU.subtract)
                            rot('pool', T4(2), T4(7), 0)
                            rot('dve', T4(3), T4(6), 1)
                            O.tt('pool', hb[:, 1, :, 0:TT], T4(2), T4(3), ALU.add)
                            if samp:
                                dst_re = hs[:, 0, q * 4:q * 4 + 4, :]
                                dst_im = hs[:, 1, q * 4:q * 4 + 4, :]
                                lastc = lambda i: Tf(i).re("p (g s t) -> p g s t", g=4, t=ST)[:, :, :, ST - 1]
                            else:
                                dst_re = hc[:, 0, q * 4:q * 4 + 4]
                                dst_im = hc[:, 1, q * 4:q * 4 + 4]
                                lastc = lambda i: T4(i)[:, :, TT - 1]
                            O.tt('dve', dst_re, lastc(0), lastc(1), ALU.subtract)
                            O.tt('dve', dst_im, lastc(2), lastc(3), ALU.add)
                            yield
                            n = 0
                            for i4 in range(4):
                                gp = q * 4 + i4
                                for ri in range(2):
                                    O.mm(ybank[:, q * 128:q * 128 + TT], Ck(ri, gp), hb[:, ri, i4, 0:TT],
                                         start=(n == 0), stop=False)
                                    n += 1
                            O.mm(ybank[:, q * 128:q * 128 + TT], Dd[:, q, :], u_bf[:, q, 0:TT], start=False, stop=True)
                            yield
                        yv = pv(ybank)
                        O.act(T4(0), yv, AF.Square)
                        O.ts('pool', T4(0), T4(0), 0.044715, ALU.mult, 1.0, ALU.add)
                        O.tt('dve', T4(0), T4(0), yv, ALU.mult)
                        O.act(T4(1), T4(0), AF.Sigmoid, scale=float(2.0 * math.sqrt(2.0 / math.pi)))
                        O.tt('dve', T4(2), yv, T4(1), ALU.mult)
                        O.cp('pool', gl_bf[:, :, 0:TT], T4(2))
                        yield
                        gbank = ps[3]
                        for oc in range(4):
                            for kc in range(4):
                                O.mm(gbank[:, oc * 128:oc * 128 + TT],
                                     Wglu.k(kc, (slice(None), kc, slice(oc * 128, (oc + 1) * 128))),
                                     gl_bf[:, kc, 0:TT], start=(kc == 0), stop=(kc == 3))
                            O.act(T4(3)[:, oc, :], pv(gbank)[:, oc, :], AF.Sigmoid, bias=prm[:, 4 + oc:5 + oc])
                        O.tt('pool', T4(2), T4(2), T4(3), ALU.mult)
                        O.tt('dve', outT[:, 0:4, 0:TT], T4(2), sza[:, :, 0:TT], ALU.mult)

                    def chainB():
                        for hd in range(4):
                            O.ts('pool', T4(10)[:, hd, :], sf[:, hd, 0:TT], prm[:, 12 + hd:13 + hd], ALU.mult,
                                 prm[:, 8 + hd:9 + hd], ALU.add)
                        O.act(Tf(11), Tf(10), AF.Ln)
                        O.ts('pool', Tf(10), Tf(10), -1.0, ALU.mult, 1.0, ALU.add)
                        yield
                        msk = (C('m4scan') if samp else C('m16scan'))[:, 0:4 * TT]
                        O.scan(Tf(13), msk, Tf(11), 0.0)
                        O.act(Tf(14), Tf(13), AF.Exp)
                        O.act(Tf(15), Tf(13), AF.Exp, scale=-1.0)
                        O.tt('dve', T4(16), sq[:, :, 0:TT], T4(14), ALU.mult)
                        O.cp('pool', qinb[:, :, 0:TT], T4(16))
                        O.tt('dve', kinb[:, :, 0:TT], T4(10), T4(15), ALU.mult)
                        yield
                        bv = Tf(13).re("p (m k) -> p m k", k=CH)
                        O.tt('pool', Tf(17).re("p (m k) -> p m k", k=CH), bv[:, :, CH - 1:CH].bc([128, 4 * NCH, CH]), bv,
                             ALU.subtract)
                        O.act(Tf(17), Tf(17), AF.Exp)
                        O.tt('dve', Tf(18), Tf(10), Tf(17), ALU.mult)
                        O.act(dec[:, 0:4 * NCH], bv[:, :, CH - 1], AF.Exp)
                        yield
                        kbank = ps[4]
                        for hd in range(4):
                            O.tr(kbank[0:TT, hd * 128:(hd + 1) * 128], T4(18)[:, hd, :], ident)
                        abank = ps[5]
                        for hd in range(4):
                            O.mm(abank[0:TT, hd * 128:hd * 128 + TT], kinb[:, hd, 0:TT], qinb[:, hd, 0:TT])
                        yield
                        am = C('att4' if samp else 'att16')[0:TT, 0:TT]
                        O.tt('dve', attT[0:TT, :, 0:TT], abank[0:TT, :].re("p (c t) -> p c t", c=4)[:, :, 0:TT],
                             am.us(1).bc([TT, 4, TT]), ALU.mult)
                        obank = ps[6]
                        cm = C('cm4' if samp else 'cm16')[0:TT, 0:NCH]
                        kvbank = ps[7]
                        for hd in range(4):
                            kx = kexp[0:TT, 0:NCH * 128].re("p (n k) -> p n k", n=NCH)
                            O.tt('dve', kx, kbank[0:TT, hd * 128:(hd + 1) * 128].us(1).bc([TT, NCH, 128]),
                                 cm.us(2).bc([TT, NCH, 128]), ALU.mult)

                            def kv_round(r_, hd=hd, kx=kx):
                                for n_ in range(4 * r_, 4 * r_ + 4):
                                    O.mm(kvbank[:, (n_ % 4) * 128:(n_ % 4 + 1) * 128], kx[:, n_, :],
                                         i_tm[0:TT, hd * 128:(hd + 1) * 128])
                            O.mm(obank[:, hd * 128:hd * 128 + TT], i_tm[0:TT, hd * 128:(hd + 1) * 128],
                                 attT[0:TT, hd, 0:TT], start=True, stop=False)
                            if samp:
                                S0h = xn[:, :].re("p (s v) -> p s v", s=8)
                                for h2 in range(2):
                                    O.dma(S0h, st_hg[j, h2 * 8:(h2 + 1) * 8, hd].rearrange("s k v -> k s v"))
                                    for n8 in range(8):
                                        n = h2 * 8 + n8
                                        O.mm(obank[:, hd * 128 + n * CH:hd * 128 + (n + 1) * CH], S0h[:, n8, :],
                                             T4(16)[:, hd, n * CH:(n + 1) * CH], start=False, stop=(n == NCH - 1))
                                    O.tt('pool', S0h, S0h,
                                         dec[:, hd * NCH + h2 * 8:hd * NCH + h2 * 8 + 8].us(2).bc([128, 8, 128]), ALU.mult)
                                    for b4 in range(2):
                                        kv_round(h2 * 2 + b4)
                                        O.tt('dve', S0h[:, b4 * 4:(b4 + 1) * 4, :], S0h[:, b4 * 4:(b4 + 1) * 4, :],
                                             kvbank[:, :].re("p (n v) -> p n v", n=4), ALU.add)
                                    out_toks.append(O.dma(s_hg[j, h2 * 8:(h2 + 1) * 8, hd].rearrange("s k v -> k s v"), S0h))
                            else:
                                for n in range(NCH):
                                    if n % 4 == 0:
                                        kv_round(n // 4)
                                        yield
                                    cur = Sst.k((hd, n % 2), (slice(None), hd, n % 2, slice(None)))
                                    nxt = Sst.k((hd, (n + 1) % 2), (slice(None), hd, (n + 1) % 2, slice(None)))
                                    O.mm(obank[:, hd * 128 + n * CH:hd * 128 + (n + 1) * CH], cur,
                                         T4(16)[:, hd, n * CH:(n + 1) * CH], start=False, stop=(n == NCH - 1))
                                    O.stt('dve', nxt, cur, dec[:, hd * NCH + n:hd * NCH + n + 1],
                                          kvbank[:, (n % 4) * 128:(n % 4 + 1) * 128], ALU.mult, ALU.add)
                        yield
                        O.cp('act', T4(10), pv(obank))
                        O.act(osq[:, :, 0:TT], pv(obank), AF.Square)
                        yield
                        nbank = ps[5]
                        for hd in range(4):
                            O.mm(nbank[:, hd * 128:hd * 128 + TT], ones_bf[:, :], osq[:, hd, 0:TT])
                        O.act(T4(11), pv(nbank), AF.Sqrt, bias=RMS_EPS)
                        O.recip(Tf(11), Tf(11))
                        O.tt('pool', Tf(10), Tf(10), Tf(11), ALU.mult)
                        O.stt('dve', outT[:, 4:8, 0:TT], T4(10), hn[:, 0:1], szb[:, :, 0:TT], ALU.mult, ALU.mult)

                    gens_ = [chainA(), chainB()] if DBG.get('even_il', True) else None
                    if gens_ is None:
                        for _ in chainA():
                            pass
                        for _ in chainB():
                            pass
                    else:
                        while gens_:
                            for g_ in list(gens_):
                                try:
                                    next(g_)
                                except StopIteration:
                                    gens_.remove(g_)

                    out_proj_and_store(layer, ti, lambda kc, half: Wout.k(kc, (slice(None), kc, slice(half * 512, (half + 1) * 512))))

                    if ti == NTILE - 2:
                        for ri in range(2):
                            O.tr(ps[0][0:16, ri * 128:(ri + 1) * 128], hc[:, ri, :], ident)
                        O.cp('dve', At(9)[0:16, 0:256], ps[0][0:16, 0:256])
                        out_toks.append(O.dma(p_re[j], At(9)[0:16, 0:128]))
                        out_toks.append(O.dma(p_im[j], At(9)[0:16, 128:256]))
                        for hd in range(4):
                            out_toks.append(O.dma(p_hg[j, hd], Sst.k((hd, 0), (slice(None), hd, 0, slice(None)))))
                    if samp:
                        for ri, dst in ((0, s_re), (1, s_im)):
                            stage = Ag(4 * ri, 4)
                            for g4 in range(4):
                                bank = ps[g4]
                                for i4 in range(4):
                                    O.tr(bank[0:16, i4 * 128:(i4 + 1) * 128], hs[:, ri, g4 * 4 + i4, :], ident)
                                O.cp('dve' if g4 % 2 else 'act', stage[0:16, g4, :], bank[0:16, :])
                            out_toks.append(O.dma(dst[j], stage[0:16, :, :].re("p a b -> p (a b)")))
                P.barrier()

        def odd_layer(layer):
            j = layer // 2
            use_vres = (j >= 1)
            DS = DECAY_SCALE
            with contextlib.ExitStack() as ls:
                def lsb(name, shape, dt=F32):
                    return sb("o%d_%s" % (layer, name), shape, dt, stack=ls)

                Wr = lsb("Wr", [128, 8, 4, 1024], BF16)
                Wo = lsb("Wo", [128, 8, 1024], BF16)
                L1 = lsb("L1", [128, 8, 320], BF16)
                L2w = lsb("L2w", [64, 1024], BF16)
                L2a = lsb("L2a", [64, 1024], BF16)
                L2v = lsb("L2v", [32, 1024], BF16)
                L2g = lsb("L2g", [128, 2, 1024], BF16)
                prm = lsb("prm", [128, 14, 8])
                xnTf = lsb("xnTf", [128, 8 * 128])
                xx = lsb("xx", [128, 8 * 128])
                mix = [lsb("mix%d" % m, [128, 8 * 128], BF16) for m in range(6)]
                lo1 = lsb("lo1", [128, 5, 128], BF16)
                carry = lsb("carry", [128, 8])
                shT = lsb("shT", [128, 8, 16])
                Hd = lsb("Hd", [128, 8, 128])
                Hb = lsb("Hb", [128, 8, 128], BF16)
                Hs = lsb("Hs", [128, 16, 128])
                gC = lsb("gC", [128, 32])
                FM_IDX = [0, 1, 2, 3, 4, 5, 6, 7, 8, 12, 13, 14, 15, 16, 17, 18, 19, 20, 21]
                SETS = []
                for si_ in range(2):
                    S_ = dict(idx=si_)
                    S_['fm'] = {i: lsb("fm%d_%d" % (si_, i), [128, 128]) for i in FM_IDX}
                    for nm, shp in (("ARd", [128, 2, 128]), ("Bd", [128, 128]), ("Kd", [128, 128]), ("BGd", [128, 128]),
                                    ("KGd", [128, 128]), ("Vfm", [128, 128]), ("GT", [128, 2, 128]), ("UVd", [128, 2, 128]),
                                    ("AMs", [128, 3, 128]), ("Qd", [128, 128]), ("PS", [128, 2, 128]), ("Wd", [128, 128]),
                                    ("WT", [128, 128])):
                        S_[nm] = lsb("%s_%d" % (nm, si_), shp)
                    S_['psA'], S_['psB'], S_['psX'], S_['psY'] = (ps[2], ps[3], ps[4], ps[5]) if si_ == 0 else (ps[0], ps[1], ps[6], ps[7])
                    SETS.append(S_)
                SETS[0]['alt'] = SETS[1]
                SETS[1]['alt'] = SETS[0]
                tmpUV = [lsb("tmpUV%d" % i, [128, 2, 128]) for i in range(4)]

                O.dma(nwb[:, :], norm_w[layer:layer + 1, :].broadcast_to([128, D]))
                stg = [xt[:, :], xn[:, :], xnTf[:, :], xx[:, :]]
                si = [0]

                def load_cast(dst_v, src_ap):
                    s_ = stg[si[0] % len(stg)]
                    si[0] += 1
                    rows, cols = src_ap.shape
                    O.dma(s_[0:rows, 0:cols], src_ap, eng=('sp', 'act')[si[0] % 2])
                    O.cp(('act', 'dve', 'pool')[si[0] % 3], dst_v, s_[0:rows, 0:cols])

                for m in range(4):
                    for kc in range(8):
                        load_cast(Wr.k((kc, m), (slice(None), kc, m, slice(None))), w_rkvz[j, m, kc * 128:(kc + 1) * 128, :])
                for kc in range(8):
                    load_cast(Wo.k(kc, (slice(None), kc, slice(None))), w_o[j, kc * 128:(kc + 1) * 128, :])
                    load_cast(L1.k(kc, (slice(None), kc, slice(None))), lora1[j, kc * 128:(kc + 1) * 128, :])
                load_cast(L2w[0:64, :], lw2[j])
                load_cast(L2a[0:64, :], la2[j])
                load_cast(L2v[0:32, :], lv2[j])
                load_cast(L2g[:, 0, :], lg2[j, 0:128, :])
                load_cast(L2g[0:32, 1, :], lg2[j, 128:160, :])
                O.dma(prm[:, :, :], prm_odd[j])
                O.memset('pool', carry[:, :], 0.0)
                for fc_ in range(8):
                    O.memset('pool', Hd.k(fc_, (slice(None), fc_, slice(None))), 0.0)
                    O.memset('pool', Hb.k(fc_, (slice(None), fc_, slice(None))), 0.0)
                for S_ in SETS:
                    for i_, t_ in enumerate((S_['ARd'][:, :, :], S_['Bd'][:, :], S_['Kd'][:, :], S_['BGd'][:, :],
                                             S_['KGd'][:, :], S_['Vfm'][:, :])):
                        O.memset('pool' if i_ % 2 else 'dve', t_, 0.0)
                O.dma(xt[0:NS, :], st_sh[j])
                for c in range(8):
                    O.tr(ps[2][:, c * 16:(c + 1) * 16], xt[0:NS, c * 128:(c + 1) * 128], ident[0:NS, 0:NS])
                O.cp('dve', shT[:, :, :], ps[2][:, 0:128].re("p (c s) -> p c s", c=8))

                vst = xn[:, :].re("p (c t) -> p c t", c=8)

                for ti in range(NTILE):
                    if ti >= DBG['tiles'] and ti != NTILE - 1:
                        continue
                    if ti == NTILE - 1 and not DBG['samp']:
                        continue
                    tok0, TT = tile_info(ti)
                    samp = (ti == NTILE - 1)
                    CC = ST if samp else 64
                    nb = TT // CC
                    nstep = 1 if samp else 2
                    xf = xnTf[:, 0:8 * TT].re("p (c t) -> p c t", c=8)
                    xxv = xx[:, 0:8 * TT].re("p (c t) -> p c t", c=8)
                    load_and_norm(layer, ti, want_f32T=xf)
                    if ti == NTILE - 2:
                        out_toks.append(O.dma(p_sh[j:j + 1, :], xn[127:128, :]))
                    if samp:
                        for sq in range(NS):
                            out_toks.append(O.dma(s_sh[j, sq:sq + 1, :], xn[sq * ST + ST - 1:sq * ST + ST, :]))
                    if use_vres:
                        O.dma(xn[:, :], vf_d[ti], rk=[('vf', ti)])
                    if DBG['stage'] < 1:
                        out_proj_and_store(layer, ti, lambda kc, half: Wo.k(kc, (slice(None), kc, slice(half * 512, (half + 1) * 512))))
                        continue
                    if not samp:
                        O.tt('dve', xxv[:, :, 1:TT], xf[:, :, 0:TT - 1], xf[:, :, 1:TT], ALU.subtract)
                        O.tt('pool', xxv[:, :, 0], carry[:, :], xf[:, :, 0], ALU.subtract)
                        O.cp('pool', carry[:, :], xf[:, :, TT - 1])
                    else:
                        xs_ = xnTf[:, 0:8 * TT].re("p (m t) -> p m t", t=ST)
                        xxs = xx[:, 0:8 * TT].re("p (m t) -> p m t", t=ST)
                        O.tt('dve', xxs[:, :, 1:ST], xs_[:, :, 0:ST - 1], xs_[:, :, 1:ST], ALU.subtract)
                        O.tt('pool', xxs[:, :, 0], shT[:, :, :].re("p c s -> p (c s)"), xs_[:, :, 0], ALU.subtract)
                    mixv = []
                    for m in range(6):
                        tmp = Hs[:, (m % 2) * 8:(m % 2) * 8 + 8, :].re("p a b -> p (a b)")[:, 0:8 * TT]
                        e1, e2 = (('dve', 'pool') if m % 2 == 0 else ('pool', 'dve'))
                        O.tt(e1, tmp.re("p (c t) -> p c t", c=8), xxv, prm[:, m, :].us(2).bc([128, 8, TT]), ALU.mult)
                        O.tt(e2, mix[m][:, 0:8 * TT], tmp, xnTf[:, 0:8 * TT], ALU.add)
                        mixv.append(mix[m][:, 0:8 * TT].re("p (c t) -> p c t", c=8))
                    slots = [(1, 0, 64, 64), (4, 64, 128, 64), (3, 128, 160, 32), (5, 160, 288, 128), (5, 288, 320, 32)]
                    for si_, (src, c0, c1, rows) in enumerate(slots):
                        if si_ == 2 and not use_vres:
                            continue
                        bank = ps[0] if si_ < 4 else ps[1]
                        cs0 = (si_ % 4) * 128
                        for kc in range(8):
                            O.mm(bank[0:rows, cs0:cs0 + TT], L1.k(kc, (slice(None), kc, slice(c0, c1))),
                                 mixv[src][:, kc, :], start=(kc == 0), stop=(kc == 7))
                        dst = lo1[0:rows, si_, 0:TT]
                        if si_ == 0:
                            O.act(dst, bank[0:rows, cs0:cs0 + TT], AF.Tanh)
                        elif si_ in (1, 2):
                            O.cp('dve', dst, bank[0:rows, cs0:cs0 + TT])
                        else:
                            O.act(dst, bank[0:rows, cs0:cs0 + TT], AF.Sigmoid)

                    def pair_gen(fc, S_):
                        fm = S_['fm']
                        ARd, Bd, Kd, BGd, KGd, Vfm, GT, UVd, AMs, Qd, PS, Wd, WT = (
                            S_[n_] for n_ in ("ARd", "Bd", "Kd", "BGd", "KGd", "Vfm", "GT", "UVd", "AMs", "Qd", "PS", "Wd", "WT"))
                        psA, psB, psX, psY = S_['psA'], S_['psB'], S_['psX'], S_['psY']
                        psN = psY
                        A_ = S_['alt']
                        PSb = PS
                        PSf = V(PSb.t[:, :, :].rearrange("p a b -> p (a b)"), ((PSb.key, 0), (PSb.key, 1)))

                        class _PS:
                            def __getitem__(self, idx):
                                return V(PSb.t[idx], ((PSb.key, idx[1]),))
                        PS = _PS()
                        fsl = slice(fc * 128, (fc + 1) * 128)

                        def F(i):
                            return fm[i][:, 0:TT]

                        def Pp(i):
                            return prm[:, i, fc:fc + 1]
                        bfm = not samp

                        def bview(buf, ncol):
                            ap = buf.t
                            flat = ap[:, :, :].rearrange("p a b -> p (a b)") if len(ap.shape) == 3 else ap[:, :]
                            return V(flat.bitcast(BF16)[:, 0:ncol], (buf.key,))
                        if bfm:
                            ARf = bview(ARd, 256)
                            A_t, R_t = ARf[:, 0:128], ARf[:, 128:256]
                            B_t, K_t = bview(Bd, 128), bview(Kd, 128)
                            AMf = bview(AMs, 384)
                            AM_ = [AMf[:, i_ * 128:(i_ + 1) * 128] for i_ in range(3)]
                            GTf = bview(GT, 256)
                            GT_ = [GTf[:, 0:128], GTf[:, 128:256]]
                            UVf_ = bview(UVd, 256)
                            U_t, Vt_t = UVf_[:, 0:128], UVf_[:, 128:256]
                            Hm = Hb.k(fc, (slice(None), fc, slice(None)))
                            BG_t, KG_t, Vf_t = bview(BGd, 128), bview(KGd, 128), bview(Vfm, 128)
                            trX = V(psX.t[:, 0:192].bitcast(BF16), (psX.key,))
                            identT = cstb[:, 256:384]
                        else:
                            BG_t, KG_t, Vf_t = BGd[:, :], KGd[:, :], Vfm[:, :]
                            trX = psX[:, 0:384]
                            identT = ident
                            ARf = ARd[:, :, :].re("p a b -> p (a b)")
                            A_t, R_t = ARd[:, 0, :], ARd[:, 1, :]
                            B_t, K_t = Bd[:, :], Kd[:, :]
                            AMf = AMs[:, :, :].re("p a b -> p (a b)")
                            AM_ = [AMs[:, i_, :] for i_ in range(3)]
                            GTf = GT[:, :, :].re("p a b -> p (a b)")
                            GT_ = [GT[:, 0, :], GT[:, 1, :]]
                            U_t, Vt_t = UVd[:, 0, :], UVd[:, 1, :]
                            Hm = None
                        Hf = Hd.k(fc, (slice(None), fc, slice(None)))

                        def Fb(i):
                            return V(fm[i].t[:, :].bitcast(BF16)[:, 0:TT], (fm[i].key,))
                        sidx_ = S_['idx']

                        def Hhalf(h2):
                            if sidx_ == 0:
                                return Hs[:, h2 * 8:(h2 + 1) * 8, :]
                            return (xx if h2 == 0 else xnTf)[:, :].re("p (s v) -> p s v", s=8)

                        def Hs_of(sq):
                            return Hhalf(sq // 8)[:, sq % 8, :]

                        def Hgrp(b4):
                            return Hhalf(b4 // 2)[:, (b4 % 2) * 4:(b4 % 2) * 4 + 4, :]
                        tbk = [psX, psY]
                        hbk = [psA, psB]

                        def h_transposes():
                            for r_ in range(2):
                                for b_ in range(2):
                                    b4 = 2 * r_ + b_
                                    for i4 in range(4):
                                        O.tr(tbk[b_][:, i4 * 128:(i4 + 1) * 128], Hs_of(b4 * 4 + i4), ident)
                                for b_ in range(2):
                                    b4 = 2 * r_ + b_
                                    O.cp('act' if b_ else 'dve', Hgrp(b4), tbk[b_][:, :].re("p (n v) -> p n v", n=4))
                                yield
                        if samp:
                            for t_ in (ARd[:, :, :], Bd[:, :], Kd[:, :], BGd[:, :], KGd[:, :], Vfm[:, :]):
                                O.memset('pool', t_, 0.0)
                            for h2 in range(2):
                                O.memset('pool', Hhalf(h2), 0.0)
                                for hh in range(2):
                                    O.dma(Hhalf(h2)[hh * 64:(hh + 1) * 64, :, hh * 64:(hh + 1) * 64],
                                          st_wkv[j, h2 * 8:(h2 + 1) * 8, 2 * fc + hh].rearrange("s v k -> v s k"))
                            yield
                            for _ in h_transposes():
                                yield
                        for m, src in ((0, 0), (1, 2), (2, 3), (3, 5)):
                            for kc in range(8):
                                O.mm(psA[:, m * 128:m * 128 + TT], Wr.k((kc, m), (slice(None), kc, m, fsl)),
                                     mixv[src][:, kc, :], start=(kc == 0), stop=(kc == 7))
                            if m == 1:
                                yield
                        O.mm(psB[:, 0:TT], L2w[0:64, fsl], lo1[0:64, 0, 0:TT])
                        O.mm(psB[:, 128:128 + TT], L2a[0:64, fsl], lo1[0:64, 1, 0:TT])
                        if use_vres:
                            O.mm(psB[:, 256:256 + TT], L2v[0:32, fsl], lo1[0:32, 2, 0:TT])
                        O.mm(psB[:, 384:384 + TT], L2g[:, 0, fsl], lo1[:, 3, 0:TT], start=True, stop=False)
                        O.mm(psB[:, 384:384 + TT], L2g[0:32, 1, fsl], lo1[0:32, 4, 0:TT], start=False, stop=True)
                        yield
                        k_ps = psA[:, 128:128 + TT]
                        v_ps = psA[:, 256:256 + TT]
                        O.act(F(0), psB[:, 0:TT], AF.Sigmoid, bias=Pp(6))
                        O.act(F(1), psB[:, 128:128 + TT], AF.Sigmoid, bias=Pp(7))
                        O.ts('dve', F(5), k_ps, Pp(9), ALU.mult)
                        if use_vres:
                            O.act(F(20), psB[:, 256:256 + TT], AF.Sigmoid, bias=Pp(8))
                            O.tt('dve', F(21), vst[:, fc, 0:TT], v_ps, ALU.subtract)
                            O.tt('pool', F(21), F(21), F(20), ALU.mult)
                            O.tt('dve', F(2), F(21), v_ps, ALU.add)
                        else:
                            O.cp('dve', F(2), v_ps)
                            O.cp('pool', vst[:, fc, 0:TT], F(2))
                        O.cp('dve', F(3), psB[:, 384:384 + TT])
                        O.act(F(4), psA[:, 384:384 + TT], AF.Sigmoid)
                        O.tt('dve', F(4), F(4), psA[:, 384:384 + TT], ALU.mult)
                        O.tt('pool', Fb(19), F(5), F(5), ALU.mult)
                        O.mm(psN[:, 0:TT], cstb[:, 0:128], Fb(19))
                        yield
                        O.scan(F(13), (C('m4scan') if samp else C('m64scan'))[:, 0:TT], F(0), 0.0)
                        O.ts('dve', F(12), F(1), -1.0, ALU.add, Pp(10), ALU.mult)
                        O.stt('dve', F(6), F(12), 1.0, k_ps, ALU.add, ALU.mult)
                        O.cp('act', F(7), psA[:, 0:TT])
                        O.act(F(14), F(13), AF.Exp, scale=-DS)
                        O.tt('pool', F(15), F(13), F(0), ALU.subtract)
                        O.act(F(15), F(15), AF.Exp, scale=-DS)
                        O.act(F(16), F(13), AF.Exp, scale=DS)
                        cv = F(13).re("p (n c) -> p n c", c=CC)
                        O.tt('pool', F(17).re("p (n c) -> p n c", c=CC), cv[:, :, CC - 1:CC].bc([128, nb, CC]), cv,
                             ALU.subtract)
                        O.act(F(17), F(17), AF.Exp, scale=-DS)
                        gCs = gC[:, S_['idx'] * 16:S_['idx'] * 16 + nb]
                        O.act(gCs, cv[:, :, CC - 1], AF.Exp, scale=-DS)
                        yield
                        O.act(F(19), psN[:, 0:TT], AF.Sqrt)
                        O.ts('dve', F(19), F(19), 1e-12, ALU.max)
                        O.recip(F(19), F(19))
                        O.tt('pool', F(5), F(5), F(19), ALU.mult)
                        O.stt('dve', Fb(12), F(7), Pp(11), F(6), ALU.mult, ALU.mult)
                        O.mm(psN[:, 128:128 + TT], cstb[:, 0:128], Fb(12))
                        O.tt('dve', F(18), F(5), F(1), ALU.mult)
                        yield
                        O.tt('dve', F(8), psN[:, 128:128 + TT], F(2), ALU.mult)

                        am = C('am_s' if samp else 'am_p')
                        slm = C('sl_s' if samp else 'sl_p')
                        for stp in range(nstep):
                            c0 = stp * 64

                            def dsts(Xv, hh):
                                rows = slice(hh * 64, (hh + 1) * 64)
                                if samp:
                                    return Xv[rows, :].re("p (s h t) -> p s h t", h=2, t=ST)[:, :, hh, :]
                                return Xv[rows, hh * 64:(hh + 1) * 64]

                            def srcs_(i, hh):
                                rows = slice(hh * 64, (hh + 1) * 64)
                                if samp:
                                    return fm[i][rows, 0:64].re("p (s t) -> p s t", t=ST)
                                return fm[i][rows, c0:c0 + 64]
                            for hh in range(2):
                                e2 = 'pool'
                                O.tt('dve' if hh == 0 else 'pool', dsts(A_t, hh), srcs_(5, hh), srcs_(15, hh), ALU.mult)
                                O.tt(e2, dsts(R_t, hh), srcs_(7, hh), srcs_(14, hh), ALU.mult)
                                O.stt('dve', dsts(B_t, hh), srcs_(18, hh), -1.0, srcs_(16, hh), ALU.mult, ALU.mult)
                                O.tt(e2, dsts(K_t, hh), srcs_(6, hh), srcs_(16, hh), ALU.mult)
                                O.stt('dve', dsts(BG_t, hh), srcs_(18, hh), -1.0, srcs_(17, hh), ALU.mult, ALU.mult)
                                O.tt(e2, dsts(KG_t, hh), srcs_(6, hh), srcs_(17, hh), ALU.mult)
                                O.cp('dve' if hh == 0 else 'pool', dsts(Vf_t, hh), srcs_(2, hh))
                            yield
                            O.mm(psX[:, 0:256], B_t, ARf)
                            O.mm(psX[:, 256:512], K_t, ARf)
                            O.mm(psY[:, 384:512], A_t, B_t)
                            yield
                            O.tt('dve', PS[:, 0, :], psX[:, 0:128], am[:, 0:128], ALU.mult)
                            O.tt('dve', AMf, psX[:, 128:512], am[:, 128:512], ALU.mult)
                            O.tt('dve', Qd[:, :], psY[:, 384:512], slm, ALU.mult)
                            O.tt('pool', PS[:, 1, :], PS[:, 0, :], ident, ALU.add)
                            O.tr(trX[:, 0:128], Vf_t, identT)
                            O.tr(trX[:, 128:256], BG_t, identT)
                            O.tr(trX[:, 256:384], KG_t, identT)
                            yield
                            O.cp('act', Vt_t, trX[:, 0:128])
                            O.cp('act', GTf, trX[:, 128:384])
                            L = 2 if samp else 6
                            for lv in range(L):
                                last = (lv == L - 1)
                                if lv == 0:
                                    if L > 2:
                                        O.mm(psY[:, 0:128], Qd[:, :], PS[:, 0, :])
                                    O.mm(psY[:, 256:384], PS[:, 0, :], Qd[:, :])
                                elif not last:
                                    O.mm(psY[:, 0:256], Qd[:, :], PSf)
                                    O.mm(psY[:, 256:384], PS[:, 0, :], Qd[:, :])
                                else:
                                    O.mm(psY[:, 128:256], Qd[:, :], PS[:, 1, :])
                                yield
                                if lv > 0:
                                    O.tt('dve', PS[:, 1, :], PS[:, 1, :], psY[:, 128:256], ALU.add)
                                if not last:
                                    if lv > 0 or L > 2:
                                        O.cp('act', PS[:, 0, :], psY[:, 0:128])
                                    O.cp('act', Qd[:, :], psY[:, 256:384])
                                yield
                            if not samp:
                                O.mm(psX[:, 0:128], A_t, Hm, start=True, stop=False)
                            else:
                                for sq in range(NS):
                                    O.mm(psX[:, 384 + sq * 8:384 + (sq + 1) * 8], Hs_of(sq),
                                         ARd[:, 0, sq * 8:(sq + 1) * 8])
                                O.cp('act', WT[:, :], psX[:, 384:512])
                                O.mm(psX[:, 0:128], WT[:, :], ident, start=True, stop=False)
                            O.mm(psX[:, 0:128], AM_[1], Vt_t, start=False, stop=True)
                            yield
                            O.cp('act', Wd[:, :], psX[:, 0:128])
                            O.mm(psX[:, 128:256], PS[:, 1, :], Wd[:, :])
                            yield
                            O.cp('dve', U_t, psX[:, 128:256])
                            O.mm(psY[:, 0:128], U_t, AM_[0], start=True, stop=False)
                            O.mm(psY[:, 0:128], Vt_t, AM_[2], start=False, stop=False)
                            if not samp:
                                O.mm(psY[:, 0:128], Hm, R_t, start=False, stop=True)
                            else:
                                for sq in range(NS):
                                    O.mm(psY[:, sq * 8:(sq + 1) * 8], Hs_of(sq), ARd[:, 1, sq * 8:(sq + 1) * 8],
                                         start=False, stop=(sq == NS - 1))
                            if not samp:
                                O.mm(psY[:, 128:256], GT_[0], U_t, start=True, stop=False)
                                O.mm(psY[:, 128:256], GT_[1], Vt_t, start=False, stop=True)
                            yield
                            for hh in range(2):
                                rows = slice(hh * 64, (hh + 1) * 64)
                                if samp:
                                    ysrc = psY[rows, 0:128].re("p (s h t) -> p s h t", h=2, t=ST)[:, :, hh, :]
                                    ydst = fm[19][rows, 0:64].re("p (s t) -> p s t", t=ST)
                                else:
                                    ysrc = psY[rows, hh * 64:(hh + 1) * 64]
                                    ydst = fm[19][rows, c0:c0 + 64]
                                O.cp('act', ydst, ysrc)
                            if not samp:
                                O.stt('dve', Hf, Hf, gCs[:, stp:stp + 1], psY[:, 128:256], ALU.mult, ALU.add)
                                O.cp('pool', Hm, Hf)
                            else:
                                UVf = UVd[:, :, :].re("p a b -> p (a b)")
                                for sq in range(NS):
                                    tb = tmpUV[sidx_ * 2 + sq % 2]
                                    if sq % 2:
                                        O.act(tb[:, :, :].re("p a b -> p (a b)"), UVf, AF.Copy,
                                              scale=C('seqsel')[:, sq:sq + 1])
                                    else:
                                        O.ts('dve', tb[:, :, :].re("p a b -> p (a b)"), UVf,
                                             C('seqsel')[:, sq:sq + 1], ALU.mult)
                                    bank = hbk[(sq // 4) % 2]
                                    cs_ = slice((sq % 4) * 128, (sq % 4 + 1) * 128)
                                    O.mm(bank[:, cs_], GT[:, 0, :], tb[:, 0, :], start=True, stop=False)
                                    O.mm(bank[:, cs_], GT[:, 1, :], tb[:, 1, :], start=False, stop=True)
                                    O.stt('dve', Hs_of(sq), Hs_of(sq), gCs[:, sq:sq + 1], bank[:, cs_],
                                          ALU.mult, ALU.add)
                                    if sq % 4 == 3:
                                        yield
                            yield
                        O.cp('pool', Fb(20), F(19))
                        O.mm(psN[:, 256:256 + TT], cstb[:, 128:256], Fb(20))
                        yield
                        O.tt('dve', F(20), F(19), psN[:, 256:256 + TT], ALU.subtract)
                        O.tt('pool', Fb(21), F(20), F(20), ALU.mult)
                        O.mm(psN[:, 384:384 + TT], cstb[:, 128:256], Fb(21))
                        yield
                        O.act(F(21), psN[:, 384:384 + TT], AF.Sqrt, bias=GN_EPS)
                        O.recip(F(21), F(21))
                        O.tt('pool', F(20), F(20), F(21), ALU.mult)
                        O.ts('dve', F(20), F(20), Pp(12), ALU.mult, Pp(13), ALU.add)
                        O.tt('pool', F(20), F(20), F(8), ALU.add)
                        O.tt('pool', F(20), F(20), F(3), ALU.mult)
                        O.tt('dve', outT[:, fc, 0:TT], F(20), F(4), ALU.mult)
                        if ti == NTILE - 2:
                            O.tr(psX[:, 0:128], Hf, ident)
                            O.cp('dve', WT[:, :], psX[:, 0:128])
                            for hh in range(2):
                                out_toks.append(O.dma(p_wkv[j, 2 * fc + hh],
                                                      WT[hh * 64:(hh + 1) * 64, hh * 64:(hh + 1) * 64]))
                        if samp:
                            for _ in h_transposes():
                                yield
                            for h2 in range(2):
                                for hh in range(2):
                                    out_toks.append(O.dma(
                                        s_wkv[j, h2 * 8:(h2 + 1) * 8, 2 * fc + hh].rearrange("s v k -> v s k"),
                                        Hhalf(h2)[hh * 64:(hh + 1) * 64, :, hh * 64:(hh + 1) * 64]))

                    if DBG['stage'] >= 2:
                        WIN = DBG.get('win_s', 2) if samp else DBG.get('win', 2)
                        active = []
                        free_sets = [0, 1][:WIN]
                        next_fc = 0
                        while next_fc < 8 or active:
                            while next_fc < 8 and free_sets:
                                sidx = free_sets.pop(0)
                                active.append((pair_gen(next_fc, SETS[sidx]), sidx))
                                next_fc += 1
                            for item in list(active):
                                try:
                                    next(item[0])
                                except StopIteration:
                                    active.remove(item)
                                    free_sets.append(item[1])
                    if not use_vres:
                        O.dma(vf_d[ti], xn[:, :], wk=[('vf', ti)])
                    out_proj_and_store(layer, ti, lambda kc, half: Wo.k(kc, (slice(None), kc, slice(half * 512, (half + 1) * 512))))
                P.barrier()

        for layer in range(n_layers):
            if layer % 2 == 0:
                even_layer(layer)
            else:
                odd_layer(layer)
        P.barrier()
        P.emit_all(st)
    return nc


def _prep_shared(inp):
    f = np.float32
    sh = {}
    sh['consts'] = CONST_ARR
    sh['norm_w'] = np.ascontiguousarray(inp['norm_w'], f)
    sh['final_norm_w'] = np.ascontiguousarray(inp['final_norm_w'], f).reshape(1, D)
    sh['even_w_in'] = np.ascontiguousarray(inp['even_w_in'], f)
    sh['even_w_out'] = np.ascontiguousarray(inp['even_w_out'], f)
    lre, lim, lst = inp['ssm_lambda_re'], inp['ssm_lambda_im'], inp['ssm_log_step']
    lst_full = np.broadcast_to(lst[:, :, None], lre.shape)
    stack = np.stack([lre, lim, lst_full], axis=1).astype(f)
    sh['lam_row'] = np.ascontiguousarray(stack.reshape(2, 3, 2048))
    sh['lam_pp'] = np.ascontiguousarray(stack.reshape(2, 3, 16, 2, 64).transpose(0, 1, 3, 4, 2).reshape(2, 3, 128, 16))
    bp = np.zeros((2, 2, 128, 2048), f)
    cp = np.zeros((2, 2, 128, 2048), f)
    for ri, (bsrc, csrc) in enumerate(((inp['ssm_b_re'], inp['ssm_c_re']), (inp['ssm_b_im'], inp['ssm_c_im']))):
        for g in range(32):
            gp, two = g // 2, g % 2
            r0 = (gp % 4) * 32 + two * 16
            c0 = gp * 128 + two * 64
            bp[:, ri, r0:r0 + 16, c0:c0 + 64] = np.transpose(bsrc[:, g], (0, 2, 1))
            cp[:, ri, two * 64:two * 64 + 64, gp * 128 + r0:gp * 128 + r0 + 16] = np.transpose(csrc[:, g], (0, 2, 1))
    sh['bpad'] = bp
    sh['cpad'] = cp
    colmaj = lambda a: np.ascontiguousarray(a.reshape(2, 4, 128).transpose(0, 2, 1), f)
    sh['ssm_d_l'] = colmaj(inp['ssm_d'].reshape(2, 512))
    sh['glu_b_l'] = colmaj(inp['ssm_glu_b'])
    sh['hlb_l'] = colmaj(inp['hgrn_lower_bounds'])
    sh['hnw_l'] = np.ascontiguousarray(inp['hgrn_norm_w'].reshape(2, 128, 1), f)
    sh['ssm_glu_w'] = np.ascontiguousarray(inp['ssm_glu_w'], f)
    sh['rw_w_rkvz'] = np.ascontiguousarray(inp['rw_w_rkvz'], f)
    sh['rw_w_o'] = np.ascontiguousarray(inp['rw_w_o'], f)
    v1 = np.zeros((2, D, 32), f)
    v1[1:] = inp['rw_v1']
    sh['lora1'] = np.ascontiguousarray(np.concatenate([inp['rw_w1'], inp['rw_a1'], v1, inp['rw_g1']], axis=-1), f)
    sh['lw2'] = np.ascontiguousarray(inp['rw_w2'], f)
    sh['la2'] = np.ascontiguousarray(inp['rw_a2'], f)
    v2 = np.zeros((2, 32, D), f)
    v2[1:] = inp['rw_v2']
    sh['lv2'] = v2
    sh['lg2'] = np.ascontiguousarray(inp['rw_g2'], f)
    v0 = np.zeros((2, D), f)
    v0[1:] = inp['rw_v0']
    vecs = [inp['rw_mix'][:, m] for m in range(6)] + [inp['rw_w0'], inp['rw_a0'], v0, inp['rw_k_k'], inp['rw_k_a'],
                                                      inp['rw_r_k'].reshape(2, D), inp['rw_ln_w'], inp['rw_ln_b']]
    pv = np.stack([np.asarray(v, f).reshape(2, 8, 128) for v in vecs], axis=1)
    sh['prm_odd'] = np.ascontiguousarray(pv.transpose(0, 3, 1, 2))
    return sh


def _in_maps(inp):
    sh = _prep_shared(inp)
    maps = []
    for c in range(NCORES):
        m = dict(sh)
        s0, s1 = c * NS, (c + 1) * NS
        m['x'] = np.ascontiguousarray(np.concatenate(
            [inp['x_prompt'][c], inp['x_sample'][s0:s1].reshape(NS * ST, D)], axis=0), np.float32)
        m['st_re'] = np.ascontiguousarray(inp['state_ssm_re'][:, s0:s1].reshape(2, NS, 2048), np.float32)
        m['st_im'] = np.ascontiguousarray(inp['state_ssm_im'][:, s0:s1].reshape(2, NS, 2048), np.float32)
        m['st_hg'] = np.ascontiguousarray(inp['state_hgrn'][:, s0:s1], np.float32)
        m['st_wkv'] = np.ascontiguousarray(inp['state_wkv'][:, s0:s1], np.float32)
        m['st_sh'] = np.ascontiguousarray(inp['state_shift'][:, s0:s1], np.float32)
        maps.append(m)
    return maps


_NC_CACHE = {}


def run(inp, n_layers=4, cores=NCORES):
    if n_layers not in _NC_CACHE:
        _NC_CACHE[n_layers] = build_nc(n_layers)
    nc = _NC_CACHE[n_layers]
    maps = _in_maps(inp)[:cores]
    res = run_bass_kernel_spmd(nc, maps, core_ids=list(range(cores)))
    return res.results


def kernel(**inputs):
    inp = {k: np.asarray(v) for k, v in inputs.items()}
    rs = run(inp, 4, NCORES)
    f = np.float32
    y_prompt = np.stack([r['y'][:SEQ] for r in rs]).astype(f)
    y_sample = np.concatenate([r['y'][SEQ:].reshape(NS, ST, D) for r in rs]).astype(f)
    cat1 = lambda k, shp: np.stack([r[k].reshape(shp) for r in rs], axis=1).astype(f)
    cats = lambda k, shp: np.concatenate([r[k].reshape(shp) for r in rs], axis=1).astype(f)
    return (y_prompt, y_sample,
            cat1('p_re', (2, 32, 64)), cat1('p_im', (2, 32, 64)), cat1('p_hg', (2, 4, 128, 128)),
            cat1('p_wkv', (2, 16, 64, 64)), cat1('p_sh', (2, D)),
            cats('s_re', (2, NS, 32, 64)), cats('s_im', (2, NS, 32, 64)), cats('s_hg', (2, NS, 4, 128, 128)),
            cats('s_wkv', (2, NS, 16, 64, 64)), cats('s_sh', (2, NS, D)))
```

```python
import contextlib
import math
import numpy as np
import concourse.bass as bass
import concourse.mybir as mybir
from concourse.bass_utils import run_bass_kernel_spmd

F32 = mybir.dt.float32
BF16 = mybir.dt.bfloat16
ALU = mybir.AluOpType
AF = mybir.ActivationFunctionType

NCORES = 8
DBG = {'stage': 9, 'tiles': 99, 'samp': True}
D = 1024
SEQ = 2048
NS = 16
ST = 4
NTOK = SEQ + NS * ST
NTILE = SEQ // 128 + 1
MAGIC = float(1.5 * 2 ** 23)
TWO_PI = float(2 * math.pi)
RMS_EPS = 1e-6
GN_EPS = 64e-5
DECAY_SCALE = math.exp(-0.5)


class V:
    __slots__ = ('ap', 'keys')

    def __init__(self, ap, keys):
        self.ap = ap
        self.keys = keys

    def __getitem__(self, idx):
        return V(self.ap[idx], self.keys)

    def re(self, pat, **kw):
        return V(self.ap.rearrange(pat, **kw), self.keys)

    def bc(self, shape):
        return V(self.ap.to_broadcast(list(shape)), self.keys)

    def us(self, axis):
        return V(self.ap.unsqueeze(axis), self.keys)


class B:
    def __init__(self, t, key):
        self.t = t
        self.key = key

    def __getitem__(self, idx):
        return V(self.t[idx], (self.key,))

    def k(self, sub, idx=slice(None)):
        return V(self.t[idx], ((self.key, sub),))


class Prog:
    ENGS = ('pe', 'act', 'dve', 'pool', 'sp')
    EPOCH = 4000
    NDSEM = 8

    def __init__(self, nc):
        self.nc = nc
        self.q = {e: [] for e in self.ENGS}
        self.cnt = {e: 0 for e in self.ENGS}
        self.seen = {e: {} for e in self.ENGS}
        self.reg = {}
        self.sems = {}
        self.dma_i = 0
        self.dma_val = [0] * self.NDSEM
        self.last = {}

    def _r(self, key):
        r = self.reg.get(key)
        if r is None:
            r = [None, {}]
            self.reg[key] = r
        return r

    def _need(self, eng, tok, waits):
        if tok is None:
            return
        sk, v = tok
        if self.seen[eng].get(sk, 0) >= v:
            return
        if v > waits.get(sk, 0):
            waits[sk] = v

    def op(self, eng, emit, reads=(), writes=(), dma=False):
        waits = {}
        for k in reads:
            r = self._r(k)
            self._need(eng, r[0], waits)
            if isinstance(k, str) and k.startswith('ps'):
                for tok in r[1].values():
                    if tok[0][0] != eng:
                        self._need(eng, tok, waits)
        for k in writes:
            r = self._r(k)
            w = r[0]
            if not (eng == 'pe' and w is not None and w[0][0] == 'pe'):
                self._need(eng, w, waits)
            for tok in r[1].values():
                self._need(eng, tok, waits)
        if dma:
            di = self.dma_i % self.NDSEM
            self.dma_i += 1
            sk = ('dma', di)
            prev = self.dma_val[di]
            if prev > 0:
                self._need(eng, (sk, prev), waits)
            self.dma_val[di] = prev + 16
            tok = (sk, prev + 16)
            inc = 16
        else:
            c = self.cnt[eng]
            sk = (eng, c // self.EPOCH)
            tok = (sk, c % self.EPOCH + 1)
            self.cnt[eng] = c + 1
            inc = 1
        for sk2, v in waits.items():
            self.seen[eng][sk2] = v
        self.q[eng].append((emit, list(waits.items()), sk, inc))
        self.last[sk] = tok
        for k in reads:
            self._r(k)[1][tok[0]] = tok
        for k in writes:
            r = self._r(k)
            r[0] = tok
            r[1] = {}
        return tok

    def barrier(self, engs=None):
        toks = list(self.last.values())
        for e in (engs or self.ENGS):
            waits = {}
            for t in toks:
                self._need(e, t, waits)
            for sk2, v in waits.items():
                self.seen[e][sk2] = v
            if waits:
                self.q[e].append((None, list(waits.items()), None, 0))

    def emit_all(self, stack):
        nc = self.nc
        for e in self.ENGS:
            for (emit, waits, sk, inc) in self.q[e]:
                for s in ([sk] if sk is not None else []) + [w[0] for w in waits]:
                    if s not in self.sems:
                        self.sems[s] = stack.enter_context(
                            nc.semaphore("s_" + "_".join(str(x) for x in s)))
        block = stack.enter_context(nc.Block())
        sems = self.sems

        def run(e):
            def body(h):
                for (emit, waits, sk, inc) in self.q[e]:
                    for sk2, v in waits:
                        h.wait_ge(sems[sk2], v)
                    if emit is not None:
                        emit(h).then_inc(sems[sk], inc)
            return body

        block.tensor(run('pe'))
        block.scalar(run('act'))
        block.vector(run('dve'))
        block.gpsimd(run('pool'))
        block.sync(run('sp'))


def _ks(*vs):
    out = []
    for v in vs:
        if isinstance(v, V):
            out.extend(v.keys)
    return out


def _a(x):
    return x.ap if isinstance(x, V) else x


class Ops:
    def __init__(self, P):
        self.P = P
        self.rr = 0

    def mm(self, out, lhsT, rhs, start=True, stop=True):
        self.P.op('pe', lambda e: e.matmul(out.ap, lhsT.ap, rhs.ap, start=start, stop=stop),
                  reads=_ks(lhsT, rhs), writes=_ks(out))

    def tr(self, out, in_, ident):
        self.P.op('pe', lambda e: e.transpose(out.ap, in_.ap, ident.ap), reads=_ks(in_, ident), writes=_ks(out))

    def act(self, out, in_, func, bias=0.0, scale=1.0, accum=None):
        kw = {}
        if accum is not None:
            kw['accum_out'] = accum.ap
        self.P.op('act', lambda e: e.activation(out.ap, in_.ap, func, bias=_a(bias), scale=_a(scale), **kw),
                  reads=_ks(in_, bias, scale), writes=_ks(out, accum))

    def tt(self, eng, out, a, b, op):
        self.P.op(eng, lambda e: e.tensor_tensor(out.ap, a.ap, b.ap, op), reads=_ks(a, b), writes=_ks(out))

    def ts(self, eng, out, a, s1, op0, s2=None, op1=None):
        if op1 is None:
            self.P.op(eng, lambda e: e.tensor_scalar(out.ap, a.ap, _a(s1), None, op0),
                      reads=_ks(a, s1), writes=_ks(out))
        else:
            self.P.op(eng, lambda e: e.tensor_scalar(out.ap, a.ap, _a(s1), _a(s2), op0, op1),
                      reads=_ks(a, s1, s2), writes=_ks(out))

    def stt(self, eng, out, a, s, b, op0, op1):
        self.P.op(eng, lambda e: e.scalar_tensor_tensor(out.ap, a.ap, _a(s), b.ap, op0, op1),
                  reads=_ks(a, s, b), writes=_ks(out))

    def scan(self, out, d0, d1, init, op0=ALU.mult, op1=ALU.add):
        self.P.op('dve', lambda e: e.tensor_tensor_scan(out.ap, d0.ap, d1.ap, _a(init), op0, op1),
                  reads=_ks(d0, d1, init), writes=_ks(out))

    def cp(self, eng, out, in_):
        if eng == 'act':
            self.P.op('act', lambda e: e.activation(out.ap, in_.ap, AF.Copy), reads=_ks(in_), writes=_ks(out))
        else:
            self.P.op(eng, lambda e: e.tensor_copy(out.ap, in_.ap), reads=_ks(in_), writes=_ks(out))

    def recip(self, out, in_):
        self.P.op('dve', lambda e: e.reciprocal(out.ap, in_.ap), reads=_ks(in_), writes=_ks(out))

    def memset(self, eng, out, val):
        self.P.op(eng, lambda e: e.memset(out.ap, val), writes=_ks(out))

    def dma(self, out, in_, eng='sp', rk=(), wk=()):
        return self.P.op(eng, lambda e: e.dma_start(out=_a(out), in_=_a(in_)),
                         reads=_ks(in_) + list(rk), writes=_ks(out) + list(wk), dma=True)

    def any_eng(self, choices=('dve', 'pool')):
        self.rr += 1
        return choices[self.rr % len(choices)]


def _consts():
    c = {}
    c['ident'] = np.eye(128, dtype=np.float32)
    col = np.arange(512)
    c['m16scan'] = np.broadcast_to((col % 16 != 0).astype(np.float32), (128, 512)).copy()
    col = np.arange(256)
    c['m4scan'] = np.broadcast_to((col % 4 != 0).astype(np.float32), (128, 256)).copy()
    s = np.arange(128)[:, None]
    t = np.arange(128)[None, :]
    c['att16'] = ((s // 16 == t // 16) & (s <= t)).astype(np.float32)
    c['att4'] = ((s // 4 == t // 4) & (s <= t)).astype(np.float32)
    c['cm16'] = (s // 16 == np.arange(8)[None, :]).astype(np.float32)
    c['cm4'] = (s // 4 == np.arange(16)[None, :]).astype(np.float32)
    c['iota1'] = np.broadcast_to((np.arange(128) + 1).astype(np.float32), (128, 128)).copy()
    for nm, C in (('p', 64), ('s', 4)):
        idx = np.arange(128)
        blk = idx // C
        pos = idx % C
        same = blk[:, None] == blk[None, :]
        c['su_' + nm] = (same & (pos[:, None] < pos[None, :])).astype(np.float32)
        c['iu_' + nm] = (same & (pos[:, None] <= pos[None, :])).astype(np.float32)
        c['sl_' + nm] = (same & (pos[:, None] > pos[None, :])).astype(np.float32)
    c['seqsel'] = (np.arange(128)[:, None] // 8 == np.arange(16)[None, :]).astype(np.float32)
    f = np.arange(128)
    c['blk64'] = (f[:, None] // 64 == f[None, :] // 64).astype(np.float32) / 64.0
    c['blk64u'] = (f[:, None] // 64 == f[None, :] // 64).astype(np.float32)
    c['m64scan'] = np.broadcast_to((np.arange(128) % 64 != 0).astype(np.float32), (128, 128)).copy()
    c['am_p'] = np.concatenate([c['su_p'], c['iu_p'], c['su_p'], c['iu_p']], axis=1)
    c['am_s'] = np.concatenate([c['su_s'], c['iu_s'], c['su_s'], c['iu_s']], axis=1)
    for k in ('su_p', 'iu_p', 'su_s', 'iu_s', 'seqsel_unused'):
        c.pop(k, None)
    offs = {}
    o = 0
    for k, v in c.items():
        offs[k] = (o, v.shape[1])
        o += v.shape[1]
    arr = np.concatenate(list(c.values()), axis=1).astype(np.float32)
    return arr, offs


CONST_ARR, CONST_OFF = _consts()


def build_nc(n_layers=4):
    nc = bass.Bass("TRN2", target_bir_lowering=False)

    def din(name, shape):
        return nc.dram_tensor(name, list(shape), F32, kind="ExternalInput").ap()

    def dout(name, shape):
        return nc.dram_tensor(name, list(shape), F32, kind="ExternalOutput").ap()

    x_in = din("x", [NTOK, D])
    st_re = din("st_re", [2, NS, 2048])
    st_im = din("st_im", [2, NS, 2048])
    st_hg = din("st_hg", [2, NS, 4, 128, 128])
    st_wkv = din("st_wkv", [2, NS, 16, 64, 64])
    st_sh = din("st_sh", [2, NS, D])
    consts_d = din("consts", list(CONST_ARR.shape))
    norm_w = din("norm_w", [4, D])
    fnorm_w = din("final_norm_w", [1, D])
    w_in = din("even_w_in", [2, D, 3072])
    w_out = din("even_w_out", [2, D, D])
    lam_pp = din("lam_pp", [2, 3, 128, 16])
    lam_row = din("lam_row", [2, 3, 2048])
    bpad = din("bpad", [2, 2, 128, 2048])
    cpad = din("cpad", [2, 2, 128, 2048])
    ssm_d = din("ssm_d_l", [2, 128, 4])
    glu_w = din("ssm_glu_w", [2, 512, 512])
    glu_b = din("glu_b_l", [2, 128, 4])
    hlb = din("hlb_l", [2, 128, 4])
    hnw = din("hnw_l", [2, 128, 1])

    w_rkvz = din("rw_w_rkvz", [2, 4, D, D])
    w_o = din("rw_w_o", [2, D, D])
    lora1 = din("lora1", [2, D, 320])
    lw2 = din("lw2", [2, 64, D])
    la2 = din("la2", [2, 64, D])
    lv2 = din("lv2", [2, 32, D])
    lg2 = din("lg2", [2, 160, D])
    prm_odd = din("prm_odd", [2, 128, 14, 8])

    y_out = dout("y", [NTOK, D])
    p_re = dout("p_re", [2, 16, 128])
    p_im = dout("p_im", [2, 16, 128])
    p_hg = dout("p_hg", [2, 4, 128, 128])
    p_wkv = dout("p_wkv", [2, 16, 64, 64])
    p_sh = dout("p_sh", [2, D])
    s_re = dout("s_re", [2, NS, 2048])
    s_im = dout("s_im", [2, NS, 2048])
    s_hg = dout("s_hg", [2, NS, 4, 128, 128])
    s_wkv = dout("s_wkv", [2, NS, 16, 64, 64])
    s_sh = dout("s_sh", [2, NS, D])

    r_a = nc.dram_tensor("r_a", [NTOK, D], F32).ap()
    r_b = nc.dram_tensor("r_b", [NTOK, D], F32).ap()
    vf_d = nc.dram_tensor("vf_d", [NTILE, 128, D], F32).ap()

    with contextlib.ExitStack() as st:
        P = Prog(nc)
        O = Ops(P)

        def sb(name, shape, dt=F32, stack=st):
            return B(stack.enter_context(nc.sbuf_tensor(name, list(shape), dt)), name)

        ps = [B(st.enter_context(nc.psum_tensor("ps%d" % i, [128, 512], F32)), "ps%d" % i) for i in range(8)]

        NC_ = CONST_ARR.shape[1]
        cst = sb("cst", [128, NC_])
        O.dma(cst[:, :], consts_d[:, :])

        def C(name, rows=128):
            o, w = CONST_OFF[name]
            return cst[0:rows, o:o + w]

        ident = C('ident')
        ones_bf = sb("ones_bf", [128, 128], BF16)
        O.memset('pool', ones_bf[:, :], 1.0 / 128)
        cstb = sb("cstb", [128, 384], BF16)
        O.cp('pool', cstb[:, 256:384], C('ident'))
        O.cp('dve', cstb[:, 0:128], C('blk64u'))
        O.cp('dve', cstb[:, 128:256], C('blk64'))
        xt = sb("xt", [128, D])
        xn = sb("xn", [128, D])
        nwb = sb("nwb", [128, D])
        fnwb = sb("fnwb", [128, D])
        O.dma(fnwb[:, :], fnorm_w[0:1, :].broadcast_to([128, D]))
        small = sb("small", [128, 8])
        outT = sb("outT", [128, 8, 128], BF16)
        xnT_box = [None]
        out_toks = []

        srcs = [x_in, r_a, r_b, r_a, r_b]
        srck = ['x', 'ra', 'rb', 'ra', 'rb']

        def tile_info(ti):
            if ti < NTILE - 1:
                return ti * 128, 128
            return SEQ, NS * ST

        def load_and_norm(layer, ti, want_f32T=None):
            tok0, TT = tile_info(ti)
            src = srcs[layer]
            O.dma(xt[0:TT, :], src[tok0:tok0 + TT, :], rk=[(srck[layer], ti)] if layer > 0 else [])
            O.act(xn[0:TT, :], xt[0:TT, :], AF.Square, accum=small[0:TT, 0:1])
            O.act(small[0:TT, 1:2], small[0:TT, 0:1], AF.Sqrt, bias=RMS_EPS, scale=1.0 / D)
            O.recip(small[0:TT, 1:2], small[0:TT, 1:2])
            O.stt('dve', xn[0:TT, :], xt[0:TT, :], small[0:TT, 1:2], nwb[0:TT, :], ALU.mult, ALU.mult)
            for half in range(2):
                bank = ps[half]
                for c4 in range(4):
                    c = half * 4 + c4
                    O.tr(bank[:, c4 * 128:c4 * 128 + TT], xn[0:TT, c * 128:(c + 1) * 128], ident[0:TT, 0:TT])
                src_v = bank[:, :].re("p (c t) -> p c t", c=4)[:, :, 0:TT]
                if want_f32T is not None:
                    O.cp('act' if half == 0 else 'dve', want_f32T[:, half * 4:half * 4 + 4, 0:TT], src_v)
                else:
                    O.cp('act' if half == 0 else 'dve', xnT_box[0][:, half * 4:half * 4 + 4, 0:TT], src_v)

        def out_proj_and_store(layer, ti, Wo):
            tok0, TT = tile_info(ti)
            last = (layer == n_layers - 1)
            for half in range(2):
                bank = ps[4 + half]
                for kc in range(8):
                    O.mm(bank[0:TT, :], outT[:, kc, 0:TT], Wo(kc, half),
                         start=(kc == 0), stop=(kc == 7))
                O.tt('dve', xt[0:TT, half * 512:(half + 1) * 512], bank[0:TT, :],
                     xt[0:TT, half * 512:(half + 1) * 512], ALU.add)
            if not last:
                dst = srcs[layer + 1]
                O.dma(dst[tok0:tok0 + TT, :], xt[0:TT, :], wk=[(srck[layer + 1], ti)])
            else:
                O.act(xn[0:TT, :], xt[0:TT, :], AF.Square, accum=small[0:TT, 2:3])
                O.act(small[0:TT, 3:4], small[0:TT, 2:3], AF.Sqrt, bias=RMS_EPS, scale=1.0 / D)
                O.recip(small[0:TT, 3:4], small[0:TT, 3:4])
                O.stt('dve', xn[0:TT, :], xt[0:TT, :], small[0:TT, 3:4], fnwb[0:TT, :], ALU.mult, ALU.mult)
                out_toks.append(O.dma(y_out[tok0:tok0 + TT, :], xn[0:TT, :]))

        def even_layer(layer):
            j = layer // 2
            with contextlib.ExitStack() as ls:
                def lsb(name, shape, dt=F32):
                    return sb("e%d_%s" % (layer, name), shape, dt, stack=ls)

                xnT = lsb("xnT", [128, 8, 128], BF16)
                xnT_box[0] = xnT
                Win = lsb("Win", [128, 8, 3072], BF16)
                Wout = lsb("Wout", [128, 8, 1024], BF16)
                Wglu = lsb("Wglu", [128, 4, 512], BF16)
                Bp = lsb("Bp", [128, 2, 2048], BF16)
                Cp = lsb("Cp", [128, 2, 2048], BF16)
                Dd = lsb("Dd", [128, 4, 128], BF16)
                prm = lsb("prm", [128, 16])
                hn = lsb("hn", [128, 2])
                Etab = lsb("Etab", [128, 2, 16, 128])
                pp = lsb("pp", [128, 10, 16])
                rho_s = lsb("rho_s", [128, 16, 64])
                hc = lsb("hc", [128, 2, 16])
                h0 = lsb("h0", [128, 2, 16, 16])
                hs = lsb("hs", [128, 2, 16, 16])
                A = lsb("A", [128, 19, 512])
                u_bf = lsb("u_bf", [128, 4, 128], BF16)
                sza = lsb("sza", [128, 4, 128])
                sq = lsb("sq", [128, 4, 128])
                sf = lsb("sf", [128, 4, 128])
                szb = lsb("szb", [128, 4, 128])
                i_tm = lsb("i_tm", [128, 512], BF16)
                hb = lsb("hb", [128, 2, 4, 128], BF16)
                gl_bf = lsb("gl_bf", [128, 4, 128], BF16)
                osq = lsb("osq", [128, 4, 128], BF16)
                qinb = lsb("qinb", [128, 4, 128], BF16)
                kinb = lsb("kinb", [128, 4, 128], BF16)
                attT = lsb("attT", [128, 4, 128], BF16)
                kexp = lsb("kexp", [128, 16 * 128], BF16)
                Sst = lsb("Sst", [128, 4, 2, 128])
                dec = lsb("dec", [128, 64])

                def At(i, w=512):
                    return V(A.t[:, i, 0:w], (("A", layer, i),))

                def Ag(i, n):
                    return V(A.t[:, i:i + n, :], tuple(("A", layer, i + q) for q in range(n)))

                O.dma(nwb[:, :], norm_w[layer:layer + 1, :].broadcast_to([128, D]))
                stg = [xt[:, :], xn[:, :]] + [Ag(2 * q_, 2).re("p a b -> p (a b)") for q_ in range(4)]
                si = [0]

                def load_cast(dst_v, src_ap, scale=None):
                    s = stg[si[0] % len(stg)]
                    si[0] += 1
                    w = dst_v.ap.shape[-1] if len(dst_v.ap.shape) == 2 else None
                    rows, cols = src_ap.shape
                    O.dma(s[0:rows, 0:cols], src_ap)
                    eng = ('act', 'dve', 'pool')[si[0] % 3]
                    if scale is None:
                        O.cp(eng, dst_v, s[0:rows, 0:cols])
                    else:
                        O.ts('dve' if eng == 'act' else eng, dst_v, s[0:rows, 0:cols], scale, ALU.mult)

                for kc in range(8):
                    for q in range(3):
                        load_cast(Win.k(kc, (slice(None), kc, slice(q * 1024, (q + 1) * 1024))),
                                  w_in[j, kc * 128:(kc + 1) * 128, q * 1024:(q + 1) * 1024])
                for kc in range(8):
                    load_cast(Wout.k(kc, (slice(None), kc, slice(None))), w_out[j, kc * 128:(kc + 1) * 128, :])
                for kc in range(4):
                    load_cast(Wglu.k(kc, (slice(None), kc, slice(None))), glu_w[j, kc * 128:(kc + 1) * 128, :])
                for ri in range(2):
                    for hf in range(2):
                        load_cast(Cp.k((ri, hf), (slice(None), ri, slice(hf * 1024, (hf + 1) * 1024))),
                                  cpad[j, ri, :, hf * 1024:(hf + 1) * 1024], scale=(-1.0 if ri == 1 else None))
                O.dma(prm[:, 0:4], ssm_d[j])
                O.dma(prm[:, 4:8], glu_b[j])
                O.dma(prm[:, 8:12], hlb[j])
                O.dma(prm[:, 12:16], hlb[0])
                O.dma(hn[:, 0:1], hnw[j])
                if j == 0:
                    O.memset('dve', prm[:, 8:12], 0.0)
                    O.memset('dve', prm[:, 12:16], 1.0)
                else:
                    O.tt('dve', prm[:, 8:12], prm[:, 8:12], prm[:, 12:16], ALU.subtract)
                    O.act(prm[:, 8:12], prm[:, 8:12], AF.Sigmoid)
                    O.ts('dve', prm[:, 12:16], prm[:, 8:12], -1.0, ALU.mult, 1.0, ALU.add)
                for oc in range(4):
                    O.ts('dve', Dd[:, oc, :], ident, prm[:, oc:oc + 1], ALU.mult)

                def s5_params(lre, lim, lst, T):
                    a1, t1, t2, t3, t4, t5, t6 = T
                    lr = lre
                    O.ts('dve', lr, lre, -1e-4, ALU.min)
                    O.act(lst, lst, AF.Exp)
                    O.tt('dve', a1, lr, lst, ALU.mult)
                    O.act(a1, a1, AF.Exp)
                    th = lst
                    O.tt('dve', th, lim, lst, ALU.mult)
                    O.ts('dve', t1, th, 1.0 / TWO_PI, ALU.mult, MAGIC, ALU.add)
                    O.ts('dve', t1, t1, MAGIC, ALU.subtract)
                    O.stt('dve', th, t1, -TWO_PI, th, ALU.mult, ALU.add)
                    O.act(t2, th, AF.Sin)
                    O.stt('dve', t1, th, -1.0, th, ALU.mult, ALU.max)
                    O.act(t1, t1, AF.Sin, bias=float(math.pi / 2), scale=-1.0)
                    O.tt('dve', t1, t1, a1, ALU.mult)
                    O.tt('dve', t2, t2, a1, ALU.mult)
                    O.tt('dve', t3, lr, lr, ALU.mult)
                    O.tt('dve', t4, lim, lim, ALU.mult)
                    O.tt('dve', t3, t3, t4, ALU.add)
                    O.recip(t3, t3)
                    O.ts('dve', t4, t1, -1.0, ALU.add)
                    O.tt('dve', t5, t4, lr, ALU.mult)
                    O.tt('dve', t6, t2, lim, ALU.mult)
                    O.tt('dve', t5, t5, t6, ALU.add)
                    O.tt('dve', t5, t5, t3, ALU.mult)
                    O.tt('dve', t6, t2, lr, ALU.mult)
                    O.tt('dve', t4, t4, lim, ALU.mult)
                    O.tt('dve', t6, t6, t4, ALU.subtract)
                    O.tt('dve', t6, t6, t3, ALU.mult)
                    return dict(rho=a1, th=th, cr=t5, ci=t6)

                for q in range(3):
                    O.dma(pp[:, q, :], lam_pp[j, q])
                r1 = s5_params(pp[:, 0, :], pp[:, 1, :], pp[:, 2, :], [pp[:, 3 + q, :] for q in range(7)])
                rho = r1['rho']
                th = r1['th']
                ang = Ag(0, 4).re("p a (g t) -> p (a g) t", g=4)
                kf2 = Ag(4, 4).re("p a (g t) -> p (a g) t", g=4)
                O.tt('dve', ang, C('iota1').us(1).bc([128, 16, 128]), th.us(2).bc([128, 16, 128]), ALU.mult)
                O.ts('dve', kf2, ang, 1.0 / TWO_PI, ALU.mult, MAGIC, ALU.add)
                O.ts('dve', kf2, kf2, MAGIC, ALU.subtract)
                O.stt('dve', ang, kf2, -TWO_PI, ang, ALU.mult, ALU.add)
                O.act(Etab[:, 1, :, :], ang, AF.Sin)
                O.stt('dve', ang, ang, -1.0, ang, ALU.mult, ALU.max)
                O.act(Etab[:, 0, :, :], ang, AF.Sin, bias=float(math.pi / 2), scale=-1.0)
                O.tt('dve', rho_s[:, :, :], rho.us(2).bc([128, 16, 64]),
                     C('m4scan')[:, 0:64].us(1).bc([128, 16, 64]), ALU.mult)
                O.memset('pool', hc[:, :, :], 0.0)
                for qq in range(4):
                    cs_ = slice(qq * 512, (qq + 1) * 512)
                    for q in range(3):
                        O.dma(At(q), lam_row[j, q:q + 1, cs_].broadcast_to([128, 512]))
                    r2 = s5_params(At(0), At(1), At(2), [At(3 + q) for q in range(7)])
                    cr, ci = r2['cr'], r2['ci']
                    bre = xt[:, 0:512]
                    bim = xt[:, 512:1024]
                    O.dma(bre, bpad[j, 0, :, cs_])
                    O.dma(bim, bpad[j, 1, :, cs_])
                    t1 = xn[:, 0:512]
                    t2 = xn[:, 512:1024]
                    O.tt('dve', t1, bre, cr, ALU.mult)
                    O.tt('pool', t2, bim, ci, ALU.mult)
                    O.tt('dve', Bp.k((0, qq), (slice(None), 0, cs_)), t1, t2, ALU.subtract)
                    O.tt('dve', t1, bim, cr, ALU.mult)
                    O.tt('pool', t2, bre, ci, ALU.mult)
                    O.tt('dve', Bp.k((1, qq), (slice(None), 1, cs_)), t1, t2, ALU.add)
                Bk = lambda ri, gp: Bp.k((ri, gp // 4), (slice(None), ri, slice(gp * 128, (gp + 1) * 128)))
                Ck = lambda ri, gp: Cp.k((ri, gp // 8), (slice(None), ri, slice(gp * 128, (gp + 1) * 128)))

                for ri, srcst in enumerate((st_re, st_im)):
                    for hf in range(2):
                        stage = xt if hf == 0 else xn
                        O.dma(stage[0:NS, :], srcst[j, :, hf * 1024:(hf + 1) * 1024])
                        bank = ps[2 + hf]
                        for g8 in range(8):
                            O.tr(bank[:, g8 * 16:(g8 + 1) * 16], stage[0:NS, g8 * 128:(g8 + 1) * 128], ident[0:NS, 0:NS])
                        O.cp('dve', h0[:, ri, hf * 8:(hf + 1) * 8, :],
                             bank[:, 0:128].re("p (g s) -> p g s", g=8))
                for hd in range(4):
                    for sl in range(2):
                        O.memset('pool', Sst.k((hd, sl), (slice(None), hd, sl, slice(None))), 0.0)

                for ti in range(NTILE):
                    tok0, TT = tile_info(ti)
                    samp = (ti == NTILE - 1)
                    CH = 4 if samp else 16
                    NCH = TT // CH
                    load_and_norm(layer, ti)
                    grp = [(0, ps[2]), (4, ps[3]), (8, ps[4]), (12, ps[5]), (20, ps[6])]
                    for (oc0, bank) in grp:
                        for o4 in range(4):
                            oc = oc0 + o4
                            for kc in range(8):
                                O.mm(bank[:, o4 * 128:o4 * 128 + TT],
                                     Win.k(kc, (slice(None), kc, slice(oc * 128, (oc + 1) * 128))),
                                     xnT[:, kc, 0:TT], start=(kc == 0), stop=(kc == 7))
                    for kc in range(8):
                        O.mm(ps[7][0:TT, :], xnT[:, kc, 0:TT],
                             Win.k(kc, (slice(None), kc, slice(16 * 128, 20 * 128))), start=(kc == 0), stop=(kc == 7))

                    def pv(bank):
                        return bank[:, :].re("p (c t) -> p c t", c=4)[:, :, 0:TT]

                    def Tf(i):
                        return At(i)[:, 0:4 * TT]

                    def T4(i):
                        return At(i)[:, 0:4 * TT].re("p (c t) -> p c t", c=4)

                    O.cp('act', u_bf[:, :, 0:TT], pv(ps[2]))
                    O.act(sza[:, :, 0:TT], pv(ps[3]), AF.Silu)
                    O.act(sq[:, :, 0:TT], pv(ps[4]), AF.Silu)
                    O.act(sf[:, :, 0:TT], pv(ps[5]), AF.Sigmoid)
                    O.act(szb[:, :, 0:TT], pv(ps[6]), AF.Silu)
                    O.cp('dve', i_tm[0:TT, :], ps[7][0:TT, :])

                    def chainA():
                        ybank = ps[2]
                        for q in range(4):
                            bre, bim = ps[0], ps[1]
                            for i4 in range(4):
                                gp = q * 4 + i4
                                O.mm(bre[:, i4 * 128:i4 * 128 + TT], Bk(0, gp), u_bf[:, q, 0:TT])
                                O.mm(bim[:, i4 * 128:i4 * 128 + TT], Bk(1, gp), u_bf[:, q, 0:TT])

                            def rot(eng, out, a, ri, q=q):
                                if not samp:
                                    O.tt(eng, out, a, Etab[:, ri, q * 4:q * 4 + 4, 0:TT], ALU.mult)
                                else:
                                    for i4 in range(4):
                                        O.tt(eng, out[:, i4, :].re("p (s t) -> p s t", t=ST),
                                             a[:, i4, :].re("p (s t) -> p s t", t=ST),
                                             Etab[:, ri, q * 4 + i4, 0:ST].us(1).bc([128, NS, ST]), ALU.mult)
                            yield
                            rot('dve', T4(0), pv(bre), 0)
                            rot('dve', T4(1), pv(bim), 1)
                            O.tt('pool', T4(4), T4(0), T4(1), ALU.add)
                            rot('dve', T4(2), pv(bim), 0)
                            rot('dve', T4(3), pv(bre), 1)
                            O.tt('pool', T4(5), T4(2), T4(3), ALU.subtract)
                            if samp:
                                for ri, gt in ((0, 4), (1, 5)):
                                    tmp = At(0)[:, 0:64].re("p (g s) -> p g s", g=4)
                                    O.tt('dve', tmp, h0[:, ri, q * 4:q * 4 + 4, :],
                                         rho[:, q * 4:q * 4 + 4].us(2).bc([128, 4, NS]), ALU.mult)
                                    gv = Tf(gt).re("p (g s t) -> p g s t", g=4, t=ST)[:, :, :, 0]
                                    O.tt('dve', gv, gv, tmp, ALU.add)
                            yield
                            for i4 in range(4):
                                gp = q * 4 + i4
                                for ri, (gt, go) in enumerate(((4, 6), (5, 7))):
                                    if samp:
                                        O.scan(T4(go)[:, i4, :], rho_s[:, gp, :], T4(gt)[:, i4, :], 0.0)
                                    else:
                                        O.scan(T4(go)[:, i4, :], rho[:, gp:gp + 1].bc([128, TT]), T4(gt)[:, i4, :],
                                               hc[:, ri, gp:gp + 1])
                            yield
                            rot('pool', T4(0), T4(6), 0)
                            rot('dve', T4(1), T4(7), 1)
                            O.tt('pool', hb[:, 0, :, 0:TT], T4(0), T4(1), ALU.subtract)
                            rot('pool', T4(2), T4(7), 0)
                            rot('dve', T4(3), T4(6), 1)
                            O.tt('pool', hb[:, 1, :, 0:TT], T4(2), T4(3), ALU.add)
                            if samp:
                                dst_re = hs[:, 0, q * 4:q * 4 + 4, :]
                                dst_im = hs[:, 1, q * 4:q * 4 + 4, :]
                                lastc = lambda i: Tf(i).re("p (g s t) -> p g s t", g=4, t=ST)[:, :, :, ST - 1]
                            else:
                                dst_re = hc[:, 0, q * 4:q * 4 + 4]
                                dst_im = hc[:, 1, q * 4:q * 4 + 4]
                                lastc = lambda i: T4(i)[:, :, TT - 1]
                            O.tt('dve', dst_re, lastc(0), lastc(1), ALU.subtract)
                            O.tt('dve', dst_im, lastc(2), lastc(3), ALU.add)
                            yield
                            n = 0
                            for i4 in range(4):
                                gp = q * 4 + i4
                                for ri in range(2):
                                    O.mm(ybank[:, q * 128:q * 128 + TT], Ck(ri, gp), hb[:, ri, i4, 0:TT],
                                         start=(n == 0), stop=False)
                                    n += 1
                            O.mm(ybank[:, q * 128:q * 128 + TT], Dd[:, q, :], u_bf[:, q, 0:TT], start=False, stop=True)
                            yield
                        yv = pv(ybank)
                        O.act(T4(0), yv, AF.Square)
                        O.ts('pool', T4(0), T4(0), 0.044715, ALU.mult, 1.0, ALU.add)
                        O.tt('dve', T4(0), T4(0), yv, ALU.mult)
                        O.act(T4(1), T4(0), AF.Sigmoid, scale=float(2.0 * math.sqrt(2.0 / math.pi)))
                        O.tt('dve', T4(2), yv, T4(1), ALU.mult)
                        O.cp('pool', gl_bf[:, :, 0:TT], T4(2))
                        yield
                        gbank = ps[3]
                        for oc in range(4):
                            for kc in range(4):
                                O.mm(gbank[:, oc * 128:oc * 128 + TT],
                                     Wglu.k(kc, (slice(None), kc, slice(oc * 128, (oc + 1) * 128))),
                                     gl_bf[:, kc, 0:TT], start=(kc == 0), stop=(kc == 3))
                            O.act(T4(3)[:, oc, :], pv(gbank)[:, oc, :], AF.Sigmoid, bias=prm[:, 4 + oc:5 + oc])
                        O.tt('pool', T4(2), T4(2), T4(3), ALU.mult)
                        O.tt('dve', outT[:, 0:4, 0:TT], T4(2), sza[:, :, 0:TT], ALU.mult)

                    def chainB():
                        for hd in range(4):
                            O.ts('pool', T4(10)[:, hd, :], sf[:, hd, 0:TT], prm[:, 12 + hd:13 + hd], ALU.mult,
                                 prm[:, 8 + hd:9 + hd], ALU.add)
                        O.act(Tf(11), Tf(10), AF.Ln)
                        O.ts('pool', Tf(10), Tf(10), -1.0, ALU.mult, 1.0, ALU.add)
                        yield
                        msk = (C('m4scan') if samp else C('m16scan'))[:, 0:4 * TT]
                        O.scan(Tf(13), msk, Tf(11), 0.0)
                        O.act(Tf(14), Tf(13), AF.Exp)
                        O.act(Tf(15), Tf(13), AF.Exp, scale=-1.0)
                        O.tt('dve', T4(16), sq[:, :, 0:TT], T4(14), ALU.mult)
                        O.cp('pool', qinb[:, :, 0:TT], T4(16))
                        O.tt('dve', kinb[:, :, 0:TT], T4(10), T4(15), ALU.mult)
                        yield
                        bv = Tf(13).re("p (m k) -> p m k", k=CH)
                        O.tt('pool', Tf(17).re("p (m k) -> p m k", k=CH), bv[:, :, CH - 1:CH].bc([128, 4 * NCH, CH]), bv,
                             ALU.subtract)
                        O.act(Tf(17), Tf(17), AF.Exp)
                        O.tt('dve', Tf(18), Tf(10), Tf(17), ALU.mult)
                        O.act(dec[:, 0:4 * NCH], bv[:, :, CH - 1], AF.Exp)
                        yield
                        kbank = ps[4]
                        for hd in range(4):
                            O.tr(kbank[0:TT, hd * 128:(hd + 1) * 128], T4(18)[:, hd, :], ident)
                        abank = ps[5]
                        for hd in range(4):
                            O.mm(abank[0:TT, hd * 128:hd * 128 + TT], kinb[:, hd, 0:TT], qinb[:, hd, 0:TT])
                        yield
                        am = C('att4' if samp else 'att16')[0:TT, 0:TT]
                        O.tt('dve', attT[0:TT, :, 0:TT], abank[0:TT, :].re("p (c t) -> p c t", c=4)[:, :, 0:TT],
                             am.us(1).bc([TT, 4, TT]), ALU.mult)
                        obank = ps[6]
                        cm = C('cm4' if samp else 'cm16')[0:TT, 0:NCH]
                        kvbank = ps[7]
                        for hd in range(4):
                            kx = kexp[0:TT, 0:NCH * 128].re("p (n k) -> p n k", n=NCH)
                            O.tt('dve', kx, kbank[0:TT, hd * 128:(hd + 1) * 128].us(1).bc([TT, NCH, 128]),
                                 cm.us(2).bc([TT, NCH, 128]), ALU.mult)

                            def kv_round(r_, hd=hd, kx=kx):
                                for n_ in range(4 * r_, 4 * r_ + 4):
                                    O.mm(kvbank[:, (n_ % 4) * 128:(n_ % 4 + 1) * 128], kx[:, n_, :],
                                         i_tm[0:TT, hd * 128:(hd + 1) * 128])
                            O.mm(obank[:, hd * 128:hd * 128 + TT], i_tm[0:TT, hd * 128:(hd + 1) * 128],
                                 attT[0:TT, hd, 0:TT], start=True, stop=False)
                            if samp:
                                S0h = xn[:, :].re("p (s v) -> p s v", s=8)
                                for h2 in range(2):
                                    O.dma(S0h, st_hg[j, h2 * 8:(h2 + 1) * 8, hd].rearrange("s k v -> k s v"))
                                    for n8 in range(8):
                                        n = h2 * 8 + n8
                                        O.mm(obank[:, hd * 128 + n * CH:hd * 128 + (n + 1) * CH], S0h[:, n8, :],
                                             T4(16)[:, hd, n * CH:(n + 1) * CH], start=False, stop=(n == NCH - 1))
                                    O.tt('pool', S0h, S0h,
                                         dec[:, hd * NCH + h2 * 8:hd * NCH + h2 * 8 + 8].us(2).bc([128, 8, 128]), ALU.mult)
                                    for b4 in range(2):
                                        kv_round(h2 * 2 + b4)
                                        O.tt('dve', S0h[:, b4 * 4:(b4 + 1) * 4, :], S0h[:, b4 * 4:(b4 + 1) * 4, :],
                                             kvbank[:, :].re("p (n v) -> p n v", n=4), ALU.add)
                                    out_toks.append(O.dma(s_hg[j, h2 * 8:(h2 + 1) * 8, hd].rearrange("s k v -> k s v"), S0h))
                            else:
                                for n in range(NCH):
                                    if n % 4 == 0:
                                        kv_round(n // 4)
                                        yield
                                    cur = Sst.k((hd, n % 2), (slice(None), hd, n % 2, slice(None)))
                                    nxt = Sst.k((hd, (n + 1) % 2), (slice(None), hd, (n + 1) % 2, slice(None)))
                                    O.mm(obank[:, hd * 128 + n * CH:hd * 128 + (n + 1) * CH], cur,
                                         T4(16)[:, hd, n * CH:(n + 1) * CH], start=False, stop=(n == NCH - 1))
                                    O.stt('dve', nxt, cur, dec[:, hd * NCH + n:hd * NCH + n + 1],
                                          kvbank[:, (n % 4) * 128:(n % 4 + 1) * 128], ALU.mult, ALU.add)
                        yield
                        O.cp('act', T4(10), pv(obank))
                        O.act(osq[:, :, 0:TT], pv(obank), AF.Square)
                        yield
                        nbank = ps[5]
                        for hd in range(4):
                            O.mm(nbank[:, hd * 128:hd * 128 + TT], ones_bf[:, :], osq[:, hd, 0:TT])
                        O.act(T4(11), pv(nbank), AF.Sqrt, bias=RMS_EPS)
                        O.recip(Tf(11), Tf(11))
                        O.tt('pool', Tf(10), Tf(10), Tf(11), ALU.mult)
                        O.stt('dve', outT[:, 4:8, 0:TT], T4(10), hn[:, 0:1], szb[:, :, 0:TT], ALU.mult, ALU.mult)

                    gens_ = [chainA(), chainB()] if DBG.get('even_il', True) else None
                    if gens_ is None:
                        for _ in chainA():
                            pass
                        for _ in chainB():
                            pass
                    else:
                        while gens_:
                            for g_ in list(gens_):
                                try:
                                    next(g_)
                                except StopIteration:
                                    gens_.remove(g_)

                    out_proj_and_store(layer, ti, lambda kc, half: Wout.k(kc, (slice(None), kc, slice(half * 512, (half + 1) * 512))))

                    if ti == NTILE - 2:
                        for ri in range(2):
                            O.tr(ps[0][0:16, ri * 128:(ri + 1) * 128], hc[:, ri, :], ident)
                        O.cp('dve', At(9)[0:16, 0:256], ps[0][0:16, 0:256])
                        out_toks.append(O.dma(p_re[j], At(9)[0:16, 0:128]))
                        out_toks.append(O.dma(p_im[j], At(9)[0:16, 128:256]))
                        for hd in range(4):
                            out_toks.append(O.dma(p_hg[j, hd], Sst.k((hd, 0), (slice(None), hd, 0, slice(None)))))
                    if samp:
                        for ri, dst in ((0, s_re), (1, s_im)):
                            stage = Ag(4 * ri, 4)
                            for g4 in range(4):
                                bank = ps[g4]
                                for i4 in range(4):
                                    O.tr(bank[0:16, i4 * 128:(i4 + 1) * 128], hs[:, ri, g4 * 4 + i4, :], ident)
                                O.cp('dve' if g4 % 2 else 'act', stage[0:16, g4, :], bank[0:16, :])
                            out_toks.append(O.dma(dst[j], stage[0:16, :, :].re("p a b -> p (a b)")))
                P.barrier()

        def odd_layer(layer):
            j = layer // 2
            use_vres = (j >= 1)
            DS = DECAY_SCALE
            with contextlib.ExitStack() as ls:
                def lsb(name, shape, dt=F32):
                    return sb("o%d_%s" % (layer, name), shape, dt, stack=ls)

                Wr = lsb("Wr", [128, 8, 4, 1024], BF16)
                Wo = lsb("Wo", [128, 8, 1024], BF16)
                L1 = lsb("L1", [128, 8, 320], BF16)
                L2w = lsb("L2w", [64, 1024], BF16)
                L2a = lsb("L2a", [64, 1024], BF16)
                L2v = lsb("L2v", [32, 1024], BF16)
                L2g = lsb("L2g", [128, 2, 1024], BF16)
                prm = lsb("prm", [128, 14, 8])
                xnTf = lsb("xnTf", [128, 8 * 128])
                xx = lsb("xx", [128, 8 * 128])
                mix = [lsb("mix%d" % m, [128, 8 * 128], BF16) for m in range(6)]
                lo1 = lsb("lo1", [128, 5, 128], BF16)
                carry = lsb("carry", [128, 8])
                shT = lsb("shT", [128, 8, 16])
                Hd = lsb("Hd", [128, 8, 128])
                Hb = lsb("Hb", [128, 8, 128], BF16)
                Hs = lsb("Hs", [128, 16, 128])
                gC = lsb("gC", [128, 32])
                FM_IDX = [0, 1, 2, 3, 4, 5, 6, 7, 8, 12, 13, 14, 15, 16, 17, 18, 19, 20, 21]
                SETS = []
                for si_ in range(2):
                    S_ = dict(idx=si_)
                    S_['fm'] = {i: lsb("fm%d_%d" % (si_, i), [128, 128]) for i in FM_IDX}
                    for nm, shp in (("ARd", [128, 2, 128]), ("Bd", [128, 128]), ("Kd", [128, 128]), ("BGd", [128, 128]),
                                    ("KGd", [128, 128]), ("Vfm", [128, 128]), ("GT", [128, 2, 128]), ("UVd", [128, 2, 128]),
                                    ("AMs", [128, 3, 128]), ("Qd", [128, 128]), ("PS", [128, 2, 128]), ("Wd", [128, 128]),
                                    ("WT", [128, 128])):
                        S_[nm] = lsb("%s_%d" % (nm, si_), shp)
                    S_['psA'], S_['psB'], S_['psX'], S_['psY'] = (ps[2], ps[3], ps[4], ps[5]) if si_ == 0 else (ps[0], ps[1], ps[6], ps[7])
                    SETS.append(S_)
                SETS[0]['alt'] = SETS[1]
                SETS[1]['alt'] = SETS[0]
                tmpUV = [lsb("tmpUV%d" % i, [128, 2, 128]) for i in range(4)]

                O.dma(nwb[:, :], norm_w[layer:layer + 1, :].broadcast_to([128, D]))
                stg = [xt[:, :], xn[:, :], xnTf[:, :], xx[:, :]]
                si = [0]

                def load_cast(dst_v, src_ap):
                    s_ = stg[si[0] % len(stg)]
                    si[0] += 1
                    rows, cols = src_ap.shape
                    O.dma(s_[0:rows, 0:cols], src_ap)
                    O.cp(('act', 'dve', 'pool')[si[0] % 3], dst_v, s_[0:rows, 0:cols])

                for m in range(4):
                    for kc in range(8):
                        load_cast(Wr.k((kc, m), (slice(None), kc, m, slice(None))), w_rkvz[j, m, kc * 128:(kc + 1) * 128, :])
                for kc in range(8):
                    load_cast(Wo.k(kc, (slice(None), kc, slice(None))), w_o[j, kc * 128:(kc + 1) * 128, :])
                    load_cast(L1.k(kc, (slice(None), kc, slice(None))), lora1[j, kc * 128:(kc + 1) * 128, :])
                load_cast(L2w[0:64, :], lw2[j])
                load_cast(L2a[0:64, :], la2[j])
                load_cast(L2v[0:32, :], lv2[j])
                load_cast(L2g[:, 0, :], lg2[j, 0:128, :])
                load_cast(L2g[0:32, 1, :], lg2[j, 128:160, :])
                O.dma(prm[:, :, :], prm_odd[j])
                O.memset('pool', carry[:, :], 0.0)
                for fc_ in range(8):
                    O.memset('pool', Hd.k(fc_, (slice(None), fc_, slice(None))), 0.0)
                    O.memset('pool', Hb.k(fc_, (slice(None), fc_, slice(None))), 0.0)
                for S_ in SETS:
                    for i_, t_ in enumerate((S_['ARd'][:, :, :], S_['Bd'][:, :], S_['Kd'][:, :], S_['BGd'][:, :],
                                             S_['KGd'][:, :], S_['Vfm'][:, :])):
                        O.memset('pool' if i_ % 2 else 'dve', t_, 0.0)
                O.dma(xt[0:NS, :], st_sh[j])
                for c in range(8):
                    O.tr(ps[2][:, c * 16:(c + 1) * 16], xt[0:NS, c * 128:(c + 1) * 128], ident[0:NS, 0:NS])
                O.cp('dve', shT[:, :, :], ps[2][:, 0:128].re("p (c s) -> p c s", c=8))

                vst = xn[:, :].re("p (c t) -> p c t", c=8)

                for ti in range(NTILE):
                    if ti >= DBG['tiles'] and ti != NTILE - 1:
                        continue
                    if ti == NTILE - 1 and not DBG['samp']:
                        continue
                    tok0, TT = tile_info(ti)
                    samp = (ti == NTILE - 1)
                    CC = ST if samp else 64
                    nb = TT // CC
                    nstep = 1 if samp else 2
                    xf = xnTf[:, 0:8 * TT].re("p (c t) -> p c t", c=8)
                    xxv = xx[:, 0:8 * TT].re("p (c t) -> p c t", c=8)
                    load_and_norm(layer, ti, want_f32T=xf)
                    if ti == NTILE - 2:
                        out_toks.append(O.dma(p_sh[j:j + 1, :], xn[127:128, :]))
                    if samp:
                        for sq in range(NS):
                            out_toks.append(O.dma(s_sh[j, sq:sq + 1, :], xn[sq * ST + ST - 1:sq * ST + ST, :]))
                    if use_vres:
                        O.dma(xn[:, :], vf_d[ti], rk=[('vf', ti)])
                    if DBG['stage'] < 1:
                        out_proj_and_store(layer, ti, lambda kc, half: Wo.k(kc, (slice(None), kc, slice(half * 512, (half + 1) * 512))))
                        continue
                    if not samp:
                        O.tt('dve', xxv[:, :, 1:TT], xf[:, :, 0:TT - 1], xf[:, :, 1:TT], ALU.subtract)
                        O.tt('pool', xxv[:, :, 0], carry[:, :], xf[:, :, 0], ALU.subtract)
                        O.cp('pool', carry[:, :], xf[:, :, TT - 1])
                    else:
                        xs_ = xnTf[:, 0:8 * TT].re("p (m t) -> p m t", t=ST)
                        xxs = xx[:, 0:8 * TT].re("p (m t) -> p m t", t=ST)
                        O.tt('dve', xxs[:, :, 1:ST], xs_[:, :, 0:ST - 1], xs_[:, :, 1:ST], ALU.subtract)
                        O.tt('pool', xxs[:, :, 0], shT[:, :, :].re("p c s -> p (c s)"), xs_[:, :, 0], ALU.subtract)
                    mixv = []
                    for m in range(6):
                        tmp = Hs[:, (m % 2) * 8:(m % 2) * 8 + 8, :].re("p a b -> p (a b)")[:, 0:8 * TT]
                        e1, e2 = (('dve', 'pool') if m % 2 == 0 else ('pool', 'dve'))
                        O.tt(e1, tmp.re("p (c t) -> p c t", c=8), xxv, prm[:, m, :].us(2).bc([128, 8, TT]), ALU.mult)
                        O.tt(e2, mix[m][:, 0:8 * TT], tmp, xnTf[:, 0:8 * TT], ALU.add)
                        mixv.append(mix[m][:, 0:8 * TT].re("p (c t) -> p c t", c=8))
                    slots = [(1, 0, 64, 64), (4, 64, 128, 64), (3, 128, 160, 32), (5, 160, 288, 128), (5, 288, 320, 32)]
                    for si_, (src, c0, c1, rows) in enumerate(slots):
                        if si_ == 2 and not use_vres:
                            continue
                        bank = ps[0] if si_ < 4 else ps[1]
                        cs0 = (si_ % 4) * 128
                        for kc in range(8):
                            O.mm(bank[0:rows, cs0:cs0 + TT], L1.k(kc, (slice(None), kc, slice(c0, c1))),
                                 mixv[src][:, kc, :], start=(kc == 0), stop=(kc == 7))
                        dst = lo1[0:rows, si_, 0:TT]
                        if si_ == 0:
                            O.act(dst, bank[0:rows, cs0:cs0 + TT], AF.Tanh)
                        elif si_ in (1, 2):
                            O.cp('dve', dst, bank[0:rows, cs0:cs0 + TT])
                        else:
                            O.act(dst, bank[0:rows, cs0:cs0 + TT], AF.Sigmoid)

                    def pair_gen(fc, S_):
                        fm = S_['fm']
                        ARd, Bd, Kd, BGd, KGd, Vfm, GT, UVd, AMs, Qd, PS, Wd, WT = (
                            S_[n_] for n_ in ("ARd", "Bd", "Kd", "BGd", "KGd", "Vfm", "GT", "UVd", "AMs", "Qd", "PS", "Wd", "WT"))
                        psA, psB, psX, psY = S_['psA'], S_['psB'], S_['psX'], S_['psY']
                        psN = psY
                        A_ = S_['alt']
                        PSb = PS
                        PSf = V(PSb.t[:, :, :].rearrange("p a b -> p (a b)"), ((PSb.key, 0), (PSb.key, 1)))

                        class _PS:
                            def __getitem__(self, idx):
                                return V(PSb.t[idx], ((PSb.key, idx[1]),))
                        PS = _PS()
                        fsl = slice(fc * 128, (fc + 1) * 128)

                        def F(i):
                            return fm[i][:, 0:TT]

                        def Pp(i):
                            return prm[:, i, fc:fc + 1]
                        bfm = not samp

                        def bview(buf, ncol):
                            ap = buf.t
                            flat = ap[:, :, :].rearrange("p a b -> p (a b)") if len(ap.shape) == 3 else ap[:, :]
                            return V(flat.bitcast(BF16)[:, 0:ncol], (buf.key,))
                        if bfm:
                            ARf = bview(ARd, 256)
                            A_t, R_t = ARf[:, 0:128], ARf[:, 128:256]
                            B_t, K_t = bview(Bd, 128), bview(Kd, 128)
                            AMf = bview(AMs, 384)
                            AM_ = [AMf[:, i_ * 128:(i_ + 1) * 128] for i_ in range(3)]
                            GTf = bview(GT, 256)
                            GT_ = [GTf[:, 0:128], GTf[:, 128:256]]
                            UVf_ = bview(UVd, 256)
                            U_t, Vt_t = UVf_[:, 0:128], UVf_[:, 128:256]
                            Hm = Hb.k(fc, (slice(None), fc, slice(None)))
                            BG_t, KG_t, Vf_t = bview(BGd, 128), bview(KGd, 128), bview(Vfm, 128)
                            trX = V(psX.t[:, 0:192].bitcast(BF16), (psX.key,))
                            identT = cstb[:, 256:384]
                        else:
                            BG_t, KG_t, Vf_t = BGd[:, :], KGd[:, :], Vfm[:, :]
                            trX = psX[:, 0:384]
                            identT = ident
                            ARf = ARd[:, :, :].re("p a b -> p (a b)")
                            A_t, R_t = ARd[:, 0, :], ARd[:, 1, :]
                            B_t, K_t = Bd[:, :], Kd[:, :]
                            AMf = AMs[:, :, :].re("p a b -> p (a b)")
                            AM_ = [AMs[:, i_, :] for i_ in range(3)]
                            GTf = GT[:, :, :].re("p a b -> p (a b)")
                            GT_ = [GT[:, 0, :], GT[:, 1, :]]
                            U_t, Vt_t = UVd[:, 0, :], UVd[:, 1, :]
                            Hm = None
                        Hf = Hd.k(fc, (slice(None), fc, slice(None)))

                        def Fb(i):
                            return V(fm[i].t[:, :].bitcast(BF16)[:, 0:TT], (fm[i].key,))
                        sidx_ = S_['idx']

                        def Hhalf(h2):
                            if sidx_ == 0:
                                return Hs[:, h2 * 8:(h2 + 1) * 8, :]
                            return (xx if h2 == 0 else xnTf)[:, :].re("p (s v) -> p s v", s=8)

                        def Hs_of(sq):
                            return Hhalf(sq // 8)[:, sq % 8, :]

                        def Hgrp(b4):
                            return Hhalf(b4 // 2)[:, (b4 % 2) * 4:(b4 % 2) * 4 + 4, :]
                        tbk = [psX, psY]
                        hbk = [psA, psB]

                        def h_transposes():
                            for r_ in range(2):
                                for b_ in range(2):
                                    b4 = 2 * r_ + b_
                                    for i4 in range(4):
                                        O.tr(tbk[b_][:, i4 * 128:(i4 + 1) * 128], Hs_of(b4 * 4 + i4), ident)
                                for b_ in range(2):
                                    b4 = 2 * r_ + b_
                                    O.cp('act' if b_ else 'dve', Hgrp(b4), tbk[b_][:, :].re("p (n v) -> p n v", n=4))
                                yield
                        if samp:
                            for t_ in (ARd[:, :, :], Bd[:, :], Kd[:, :], BGd[:, :], KGd[:, :], Vfm[:, :]):
                                O.memset('pool', t_, 0.0)
                            for h2 in range(2):
                                O.memset('pool', Hhalf(h2), 0.0)
                                for hh in range(2):
                                    O.dma(Hhalf(h2)[hh * 64:(hh + 1) * 64, :, hh * 64:(hh + 1) * 64],
                                          st_wkv[j, h2 * 8:(h2 + 1) * 8, 2 * fc + hh].rearrange("s v k -> v s k"))
                            yield
                            for _ in h_transposes():
                                yield
                        for m, src in ((0, 0), (1, 2), (2, 3), (3, 5)):
                            for kc in range(8):
                                O.mm(psA[:, m * 128:m * 128 + TT], Wr.k((kc, m), (slice(None), kc, m, fsl)),
                                     mixv[src][:, kc, :], start=(kc == 0), stop=(kc == 7))
                            if m == 1:
                                yield
                        O.mm(psB[:, 0:TT], L2w[0:64, fsl], lo1[0:64, 0, 0:TT])
                        O.mm(psB[:, 128:128 + TT], L2a[0:64, fsl], lo1[0:64, 1, 0:TT])
                        if use_vres:
                            O.mm(psB[:, 256:256 + TT], L2v[0:32, fsl], lo1[0:32, 2, 0:TT])
                        O.mm(psB[:, 384:384 + TT], L2g[:, 0, fsl], lo1[:, 3, 0:TT], start=True, stop=False)
                        O.mm(psB[:, 384:384 + TT], L2g[0:32, 1, fsl], lo1[0:32, 4, 0:TT], start=False, stop=True)
                        yield
                        k_ps = psA[:, 128:128 + TT]
                        v_ps = psA[:, 256:256 + TT]
                        O.act(F(0), psB[:, 0:TT], AF.Sigmoid, bias=Pp(6))
                        O.act(F(1), psB[:, 128:128 + TT], AF.Sigmoid, bias=Pp(7))
                        O.ts('dve', F(5), k_ps, Pp(9), ALU.mult)
                        if use_vres:
                            O.act(F(20), psB[:, 256:256 + TT], AF.Sigmoid, bias=Pp(8))
                            O.tt('dve', F(21), vst[:, fc, 0:TT], v_ps, ALU.subtract)
                            O.tt('pool', F(21), F(21), F(20), ALU.mult)
                            O.tt('dve', F(2), F(21), v_ps, ALU.add)
                        else:
                            O.cp('dve', F(2), v_ps)
                            O.cp('pool', vst[:, fc, 0:TT], F(2))
                        O.cp('dve', F(3), psB[:, 384:384 + TT])
                        O.act(F(4), psA[:, 384:384 + TT], AF.Sigmoid)
                        O.tt('dve', F(4), F(4), psA[:, 384:384 + TT], ALU.mult)
                        O.tt('pool', Fb(19), F(5), F(5), ALU.mult)
                        O.mm(psN[:, 0:TT], cstb[:, 0:128], Fb(19))
                        yield
                        O.scan(F(13), (C('m4scan') if samp else C('m64scan'))[:, 0:TT], F(0), 0.0)
                        O.ts('dve', F(12), F(1), -1.0, ALU.add, Pp(10), ALU.mult)
                        O.stt('dve', F(6), F(12), 1.0, k_ps, ALU.add, ALU.mult)
                        O.cp('act', F(7), psA[:, 0:TT])
                        O.act(F(14), F(13), AF.Exp, scale=-DS)
                        O.tt('pool', F(15), F(13), F(0), ALU.subtract)
                        O.act(F(15), F(15), AF.Exp, scale=-DS)
                        O.act(F(16), F(13), AF.Exp, scale=DS)
                        cv = F(13).re("p (n c) -> p n c", c=CC)
                        O.tt('pool', F(17).re("p (n c) -> p n c", c=CC), cv[:, :, CC - 1:CC].bc([128, nb, CC]), cv,
                             ALU.subtract)
                        O.act(F(17), F(17), AF.Exp, scale=-DS)
                        gCs = gC[:, S_['idx'] * 16:S_['idx'] * 16 + nb]
                        O.act(gCs, cv[:, :, CC - 1], AF.Exp, scale=-DS)
                        yield
                        O.act(F(19), psN[:, 0:TT], AF.Sqrt)
                        O.ts('dve', F(19), F(19), 1e-12, ALU.max)
                        O.recip(F(19), F(19))
                        O.tt('pool', F(5), F(5), F(19), ALU.mult)
                        O.stt('dve', Fb(12), F(7), Pp(11), F(6), ALU.mult, ALU.mult)
                        O.mm(psN[:, 128:128 + TT], cstb[:, 0:128], Fb(12))
                        O.tt('dve', F(18), F(5), F(1), ALU.mult)
                        yield
                        O.tt('dve', F(8), psN[:, 128:128 + TT], F(2), ALU.mult)

                        am = C('am_s' if samp else 'am_p')
                        slm = C('sl_s' if samp else 'sl_p')
                        for stp in range(nstep):
                            c0 = stp * 64

                            def dsts(Xv, hh):
                                rows = slice(hh * 64, (hh + 1) * 64)
                                if samp:
                                    return Xv[rows, :].re("p (s h t) -> p s h t", h=2, t=ST)[:, :, hh, :]
                                return Xv[rows, hh * 64:(hh + 1) * 64]

                            def srcs_(i, hh):
                                rows = slice(hh * 64, (hh + 1) * 64)
                                if samp:
                                    return fm[i][rows, 0:64].re("p (s t) -> p s t", t=ST)
                                return fm[i][rows, c0:c0 + 64]
                            for hh in range(2):
                                e2 = 'pool'
                                O.tt('dve' if hh == 0 else 'pool', dsts(A_t, hh), srcs_(5, hh), srcs_(15, hh), ALU.mult)
                                O.tt(e2, dsts(R_t, hh), srcs_(7, hh), srcs_(14, hh), ALU.mult)
                                O.stt('dve', dsts(B_t, hh), srcs_(18, hh), -1.0, srcs_(16, hh), ALU.mult, ALU.mult)
                                O.tt(e2, dsts(K_t, hh), srcs_(6, hh), srcs_(16, hh), ALU.mult)
                                O.stt('dve', dsts(BG_t, hh), srcs_(18, hh), -1.0, srcs_(17, hh), ALU.mult, ALU.mult)
                                O.tt(e2, dsts(KG_t, hh), srcs_(6, hh), srcs_(17, hh), ALU.mult)
                                O.cp('dve' if hh == 0 else 'pool', dsts(Vf_t, hh), srcs_(2, hh))
                            yield
                            O.mm(psX[:, 0:256], B_t, ARf)
                            O.mm(psX[:, 256:512], K_t, ARf)
                            O.mm(psY[:, 384:512], A_t, B_t)
                            yield
                            O.tt('dve', PS[:, 0, :], psX[:, 0:128], am[:, 0:128], ALU.mult)
                            O.tt('dve', AMf, psX[:, 128:512], am[:, 128:512], ALU.mult)
                            O.tt('dve', Qd[:, :], psY[:, 384:512], slm, ALU.mult)
                            O.tt('pool', PS[:, 1, :], PS[:, 0, :], ident, ALU.add)
                            O.tr(trX[:, 0:128], Vf_t, identT)
                            O.tr(trX[:, 128:256], BG_t, identT)
                            O.tr(trX[:, 256:384], KG_t, identT)
                            yield
                            O.cp('act', Vt_t, trX[:, 0:128])
                            O.cp('act', GTf, trX[:, 128:384])
                            L = 2 if samp else 6
                            for lv in range(L):
                                last = (lv == L - 1)
                                if lv == 0:
                                    if L > 2:
                                        O.mm(psY[:, 0:128], Qd[:, :], PS[:, 0, :])
                                    O.mm(psY[:, 256:384], PS[:, 0, :], Qd[:, :])
                                elif not last:
                                    O.mm(psY[:, 0:256], Qd[:, :], PSf)
                                    O.mm(psY[:, 256:384], PS[:, 0, :], Qd[:, :])
                                else:
                                    O.mm(psY[:, 128:256], Qd[:, :], PS[:, 1, :])
                                yield
                                if lv > 0:
                                    O.tt('dve', PS[:, 1, :], PS[:, 1, :], psY[:, 128:256], ALU.add)
                                if not last:
                                    if lv > 0 or L > 2:
                                        O.cp('act', PS[:, 0, :], psY[:, 0:128])
                                    O.cp('act', Qd[:, :], psY[:, 256:384])
                                yield
                            if not samp:
                                O.mm(psX[:, 0:128], A_t, Hm, start=True, stop=False)
                            else:
                                for sq in range(NS):
                                    O.mm(psX[:, 384 + sq * 8:384 + (sq + 1) * 8], Hs_of(sq),
                                         ARd[:, 0, sq * 8:(sq + 1) * 8])
                                O.cp('act', WT[:, :], psX[:, 384:512])
                                O.mm(psX[:, 0:128], WT[:, :], ident, start=True, stop=False)
                            O.mm(psX[:, 0:128], AM_[1], Vt_t, start=False, stop=True)
                            yield
                            O.cp('act', Wd[:, :], psX[:, 0:128])
                            O.mm(psX[:, 128:256], PS[:, 1, :], Wd[:, :])
                            yield
                            O.cp('dve', U_t, psX[:, 128:256])
                            O.mm(psY[:, 0:128], U_t, AM_[0], start=True, stop=False)
                            O.mm(psY[:, 0:128], Vt_t, AM_[2], start=False, stop=False)
                            if not samp:
                                O.mm(psY[:, 0:128], Hm, R_t, start=False, stop=True)
                            else:
                                for sq in range(NS):
                                    O.mm(psY[:, sq * 8:(sq + 1) * 8], Hs_of(sq), ARd[:, 1, sq * 8:(sq + 1) * 8],
                                         start=False, stop=(sq == NS - 1))
                            if not samp:
                                O.mm(psY[:, 128:256], GT_[0], U_t, start=True, stop=False)
                                O.mm(psY[:, 128:256], GT_[1], Vt_t, start=False, stop=True)
                            yield
                            for hh in range(2):
                                rows = slice(hh * 64, (hh + 1) * 64)
                                if samp:
                                    ysrc = psY[rows, 0:128].re("p (s h t) -> p s h t", h=2, t=ST)[:, :, hh, :]
                                    ydst = fm[19][rows, 0:64].re("p (s t) -> p s t", t=ST)
                                else:
                                    ysrc = psY[rows, hh * 64:(hh + 1) * 64]
                                    ydst = fm[19][rows, c0:c0 + 64]
                                O.cp('act', ydst, ysrc)
                            if not samp:
                                O.stt('dve', Hm, Hf, gCs[:, stp:stp + 1], psY[:, 128:256], ALU.mult, ALU.add)
                                O.stt('dve', Hf, Hf, gCs[:, stp:stp + 1], psY[:, 128:256], ALU.mult, ALU.add)
                            else:
                                UVf = UVd[:, :, :].re("p a b -> p (a b)")
                                for sq in range(NS):
                                    tb = tmpUV[sidx_ * 2 + sq % 2]
                                    if sq % 2:
                                        O.act(tb[:, :, :].re("p a b -> p (a b)"), UVf, AF.Copy,
                                              scale=C('seqsel')[:, sq:sq + 1])
                                    else:
                                        O.ts('dve', tb[:, :, :].re("p a b -> p (a b)"), UVf,
                                             C('seqsel')[:, sq:sq + 1], ALU.mult)
                                    bank = hbk[(sq // 4) % 2]
                                    cs_ = slice((sq % 4) * 128, (sq % 4 + 1) * 128)
                                    O.mm(bank[:, cs_], GT[:, 0, :], tb[:, 0, :], start=True, stop=False)
                                    O.mm(bank[:, cs_], GT[:, 1, :], tb[:, 1, :], start=False, stop=True)
                                    O.stt('dve', Hs_of(sq), Hs_of(sq), gCs[:, sq:sq + 1], bank[:, cs_],
                                          ALU.mult, ALU.add)
                                    if sq % 4 == 3:
                                        yield
                            yield
                        O.cp('pool', Fb(20), F(19))
                        O.mm(psN[:, 256:256 + TT], cstb[:, 128:256], Fb(20))
                        yield
                        O.tt('dve', F(20), F(19), psN[:, 256:256 + TT], ALU.subtract)
                        O.tt('pool', Fb(21), F(20), F(20), ALU.mult)
                        O.mm(psN[:, 384:384 + TT], cstb[:, 128:256], Fb(21))
                        yield
                        O.act(F(21), psN[:, 384:384 + TT], AF.Sqrt, bias=GN_EPS)
                        O.recip(F(21), F(21))
                        O.tt('pool', F(20), F(20), F(21), ALU.mult)
                        O.ts('dve', F(20), F(20), Pp(12), ALU.mult, Pp(13), ALU.add)
                        O.tt('pool', F(20), F(20), F(8), ALU.add)
                        O.tt('pool', F(20), F(20), F(3), ALU.mult)
                        O.tt('dve', outT[:, fc, 0:TT], F(20), F(4), ALU.mult)
                        if ti == NTILE - 2:
                            O.tr(psX[:, 0:128], Hf, ident)
                            O.cp('dve', WT[:, :], psX[:, 0:128])
                            for hh in range(2):
                                out_toks.append(O.dma(p_wkv[j, 2 * fc + hh],
                                                      WT[hh * 64:(hh + 1) * 64, hh * 64:(hh + 1) * 64]))
                        if samp:
                            for _ in h_transposes():
                                yield
                            for h2 in range(2):
                                for hh in range(2):
                                    out_toks.append(O.dma(
                                        s_wkv[j, h2 * 8:(h2 + 1) * 8, 2 * fc + hh].rearrange("s v k -> v s k"),
                                        Hhalf(h2)[hh * 64:(hh + 1) * 64, :, hh * 64:(hh + 1) * 64]))

                    if DBG['stage'] >= 2:
                        WIN = DBG.get('win_s', 2) if samp else DBG.get('win', 2)
                        active = []
                        free_sets = [0, 1][:WIN]
                        next_fc = 0
                        while next_fc < 8 or active:
                            while next_fc < 8 and free_sets:
                                sidx = free_sets.pop(0)
                                active.append((pair_gen(next_fc, SETS[sidx]), sidx))
                                next_fc += 1
                            for item in list(active):
                                try:
                                    next(item[0])
                                except StopIteration:
                                    active.remove(item)
                                    free_sets.append(item[1])
                    if not use_vres:
                        O.dma(vf_d[ti], xn[:, :], wk=[('vf', ti)])
                    out_proj_and_store(layer, ti, lambda kc, half: Wo.k(kc, (slice(None), kc, slice(half * 512, (half + 1) * 512))))
                P.barrier()

        for layer in range(n_layers):
            if layer % 2 == 0:
                even_layer(layer)
            else:
                odd_layer(layer)
        P.barrier()
        P.emit_all(st)
    return nc


def _prep_shared(inp):
    f = np.float32
    sh = {}
    sh['consts'] = CONST_ARR
    sh['norm_w'] = np.ascontiguousarray(inp['norm_w'], f)
    sh['final_norm_w'] = np.ascontiguousarray(inp['final_norm_w'], f).reshape(1, D)
    sh['even_w_in'] = np.ascontiguousarray(inp['even_w_in'], f)
    sh['even_w_out'] = np.ascontiguousarray(inp['even_w_out'], f)
    lre, lim, lst = inp['ssm_lambda_re'], inp['ssm_lambda_im'], inp['ssm_log_step']
    lst_full = np.broadcast_to(lst[:, :, None], lre.shape)
    stack = np.stack([lre, lim, lst_full], axis=1).astype(f)
    sh['lam_row'] = np.ascontiguousarray(stack.reshape(2, 3, 2048))
    sh['lam_pp'] = np.ascontiguousarray(stack.reshape(2, 3, 16, 2, 64).transpose(0, 1, 3, 4, 2).reshape(2, 3, 128, 16))
    bp = np.zeros((2, 2, 128, 2048), f)
    cp = np.zeros((2, 2, 128, 2048), f)
    for ri, (bsrc, csrc) in enumerate(((inp['ssm_b_re'], inp['ssm_c_re']), (inp['ssm_b_im'], inp['ssm_c_im']))):
        for g in range(32):
            gp, two = g // 2, g % 2
            r0 = (gp % 4) * 32 + two * 16
            c0 = gp * 128 + two * 64
            bp[:, ri, r0:r0 + 16, c0:c0 + 64] = np.transpose(bsrc[:, g], (0, 2, 1))
            cp[:, ri, two * 64:two * 64 + 64, gp * 128 + r0:gp * 128 + r0 + 16] = np.transpose(csrc[:, g], (0, 2, 1))
    sh['bpad'] = bp
    sh['cpad'] = cp
    colmaj = lambda a: np.ascontiguousarray(a.reshape(2, 4, 128).transpose(0, 2, 1), f)
    sh['ssm_d_l'] = colmaj(inp['ssm_d'].reshape(2, 512))
    sh['glu_b_l'] = colmaj(inp['ssm_glu_b'])
    sh['hlb_l'] = colmaj(inp['hgrn_lower_bounds'])
    sh['hnw_l'] = np.ascontiguousarray(inp['hgrn_norm_w'].reshape(2, 128, 1), f)
    sh['ssm_glu_w'] = np.ascontiguousarray(inp['ssm_glu_w'], f)
    sh['rw_w_rkvz'] = np.ascontiguousarray(inp['rw_w_rkvz'], f)
    sh['rw_w_o'] = np.ascontiguousarray(inp['rw_w_o'], f)
    v1 = np.zeros((2, D, 32), f)
    v1[1:] = inp['rw_v1']
    sh['lora1'] = np.ascontiguousarray(np.concatenate([inp['rw_w1'], inp['rw_a1'], v1, inp['rw_g1']], axis=-1), f)
    sh['lw2'] = np.ascontiguousarray(inp['rw_w2'], f)
    sh['la2'] = np.ascontiguousarray(inp['rw_a2'], f)
    v2 = np.zeros((2, 32, D), f)
    v2[1:] = inp['rw_v2']
    sh['lv2'] = v2
    sh['lg2'] = np.ascontiguousarray(inp['rw_g2'], f)
    v0 = np.zeros((2, D), f)
    v0[1:] = inp['rw_v0']
    vecs = [inp['rw_mix'][:, m] for m in range(6)] + [inp['rw_w0'], inp['rw_a0'], v0, inp['rw_k_k'], inp['rw_k_a'],
                                                      inp['rw_r_k'].reshape(2, D), inp['rw_ln_w'], inp['rw_ln_b']]
    pv = np.stack([np.asarray(v, f).reshape(2, 8, 128) for v in vecs], axis=1)
    sh['prm_odd'] = np.ascontiguousarray(pv.transpose(0, 3, 1, 2))
    return sh


def _in_maps(inp):
    sh = _prep_shared(inp)
    maps = []
    for c in range(NCORES):
        m = dict(sh)
        s0, s1 = c * NS, (c + 1) * NS
        m['x'] = np.ascontiguousarray(np.concatenate(
            [inp['x_prompt'][c], inp['x_sample'][s0:s1].reshape(NS * ST, D)], axis=0), np.float32)
        m['st_re'] = np.ascontiguousarray(inp['state_ssm_re'][:, s0:s1].reshape(2, NS, 2048), np.float32)
        m['st_im'] = np.ascontiguousarray(inp['state_ssm_im'][:, s0:s1].reshape(2, NS, 2048), np.float32)
        m['st_hg'] = np.ascontiguousarray(inp['state_hgrn'][:, s0:s1], np.float32)
        m['st_wkv'] = np.ascontiguousarray(inp['state_wkv'][:, s0:s1], np.float32)
        m['st_sh'] = np.ascontiguousarray(inp['state_shift'][:, s0:s1], np.float32)
        maps.append(m)
    return maps


_NC_CACHE = {}


def run(inp, n_layers=4, cores=NCORES):
    if n_layers not in _NC_CACHE:
        _NC_CACHE[n_layers] = build_nc(n_layers)
    nc = _NC_CACHE[n_layers]
    maps = _in_maps(inp)[:cores]
    res = run_bass_kernel_spmd(nc, maps, core_ids=list(range(cores)))
    return res.results


def kernel(**inputs):
    inp = {k: np.asarray(v) for k, v in inputs.items()}
    rs = run(inp, 4, NCORES)
    f = np.float32
    y_prompt = np.stack([r['y'][:SEQ] for r in rs]).astype(f)
    y_sample = np.concatenate([r['y'][SEQ:].reshape(NS, ST, D) for r in rs]).astype(f)
    cat1 = lambda k, shp: np.stack([r[k].reshape(shp) for r in rs], axis=1).astype(f)
    cats = lambda k, shp: np.concatenate([r[k].reshape(shp) for r in rs], axis=1).astype(f)
    return (y_prompt, y_sample,
            cat1('p_re', (2, 32, 64)), cat1('p_im', (2, 32, 64)), cat1('p_hg', (2, 4, 128, 128)),
            cat1('p_wkv', (2, 16, 64, 64)), cat1('p_sh', (2, D)),
            cats('s_re', (2, NS, 32, 64)), cats('s_im', (2, NS, 32, 64)), cats('s_hg', (2, NS, 4, 128, 128)),
            cats('s_wkv', (2, NS, 16, 64, 64)), cats('s_sh', (2, NS, D)))
```
